# Optimizing a Trainium2 kernel written in Bass

```python
import math
import jax, jax.numpy as jnp
from jax import lax
import numpy as np

D_MODEL = 1024
BATCH = 16
SEQ = 4096
DEPTH = 4

HEAD_DIM = 64
BLOCK = 128
A_HEADS = D_MODEL // (2 * HEAD_DIM)
A_KV_HEADS = 2
A_GROUP = A_HEADS // A_KV_HEADS
WINDOW = 128
B_HEADS = D_MODEL // (4 * HEAD_DIM)
B_VDIM = 2 * HEAD_DIM
C_HEADS = D_MODEL // HEAD_DIM
N_BUCKETS = 32
MAX_EXACT = N_BUCKETS // 2
MAX_DISTANCE = 128
SOFT_HEADS = A_HEADS + B_HEADS
A_Q = A_HEADS * HEAD_DIM
A_KV = A_KV_HEADS * HEAD_DIM
B_QK = B_HEADS * 2 * HEAD_DIM
B_V = B_HEADS * B_VDIM
EVEN_IN = A_Q + 2 * A_KV + 2 * B_QK + B_V
EVEN_OUT = A_Q + B_V
EVEN_SPLITS = [A_Q, A_Q + A_KV, A_Q + 2 * A_KV, A_Q + 2 * A_KV + B_QK, A_Q + 2 * A_KV + 2 * B_QK]
C_WIDTH = C_HEADS * HEAD_DIM
ODD_IN = 3 * C_WIDTH
D_FF = 2816
CONV_W = 3
N_EVEN = (DEPTH + 1) // 2
N_ODD = DEPTH // 2
EPS = 1e-6
SCALE = HEAD_DIM ** -0.5

kernel_name = "hybrid_swa_sink_diff_stickbreak_convffn"


def rms_norm(x, g):
    x32 = x.astype(jnp.float32)
    y = x32 * lax.rsqrt(jnp.mean(x32 * x32, axis=-1, keepdims=True) + EPS)
    return (y * g.astype(jnp.float32)).astype(x.dtype)


def t5_bucket(dist):
    n = jnp.maximum(dist, 0)
    nf = jnp.maximum(n, 1).astype(jnp.float32)
    large = MAX_EXACT + (jnp.log(nf / MAX_EXACT) / math.log(MAX_DISTANCE / MAX_EXACT)
                         * (N_BUCKETS - MAX_EXACT)).astype(jnp.int32)
    large = jnp.minimum(large, N_BUCKETS - 1)
    return jnp.where(n < MAX_EXACT, n, large)


def sliding_window_sink_attention(q, k, v, sinks, bias_table):
    bsz, seq = q.shape[:2]
    nb = seq // BLOCK
    qb = q.reshape(bsz, nb, BLOCK, A_KV_HEADS, A_GROUP, HEAD_DIM)

    def band(t):
        tb = t.reshape(bsz, nb, BLOCK, A_KV_HEADS, HEAD_DIM)
        prev = jnp.concatenate([jnp.zeros_like(tb[:, :1]), tb[:, :-1]], axis=1)
        return jnp.concatenate([prev, tb], axis=2)

    kb, vb = band(k), band(v)
    qi = jnp.arange(BLOCK)[:, None]
    kj = jnp.arange(2 * BLOCK)[None, :]
    dist = qi + BLOCK - kj
    in_window = (dist >= 0) & (dist < WINDOW)
    blk = jnp.arange(nb)[:, None, None]
    valid = in_window[None] & (blk * BLOCK - BLOCK + kj[None] >= 0)
    bias = bias_table[t5_bucket(dist)]
    bias = bias.reshape(BLOCK, 2 * BLOCK, A_KV_HEADS, A_GROUP).transpose(2, 3, 0, 1)
    s = jnp.einsum('bnqhgd,bnkhd->bnhgqk', qb, kb).astype(jnp.float32) * SCALE + bias
    s = jnp.where(valid[None, :, None, None], s, -jnp.inf)
    sink = sinks.astype(jnp.float32).reshape(A_KV_HEADS, A_GROUP)[None, None, :, :, None, None]
    m = jnp.maximum(jnp.max(s, axis=-1, keepdims=True), sink)
    p = jnp.exp(s - m)
    w = p / (jnp.sum(p, axis=-1, keepdims=True) + jnp.exp(sink - m))
    o = jnp.einsum('bnhgqk,bnkhd->bnqhgd', w.astype(v.dtype), vb)
    return o.reshape(bsz, seq, A_Q)


def differential_attention(q1, q2, k1, k2, v, lam, lambda_init, bias_table, gain):
    bsz, seq = q1.shape[:2]
    nb = seq // BLOCK
    kpos = jnp.arange(seq)

    def block(i):
        start = i * BLOCK
        qpos = start + jnp.arange(BLOCK)
        dist = qpos[:, None] - kpos[None, :]
        causal = dist >= 0
        bias = bias_table[t5_bucket(dist)].transpose(2, 0, 1)

        def probs(q, k):
            qb = lax.dynamic_slice_in_dim(q, start, BLOCK, axis=1)
            s = jnp.einsum('bqhd,bkhd->bhqk', qb, k).astype(jnp.float32) * SCALE + bias
            return jax.nn.softmax(jnp.where(causal, s, -jnp.inf), axis=-1)

        w = probs(q1, k1) - lam * probs(q2, k2)
        return jnp.einsum('bhqk,bkhe->bqhe', w.astype(v.dtype), v)

    o = lax.map(block, jnp.arange(nb))
    o = o.transpose(1, 0, 2, 3, 4).reshape(bsz, seq, B_HEADS, B_VDIM)
    o = rms_norm(o, gain) * (1.0 - lambda_init)
    return o.reshape(bsz, seq, B_V)


def stick_breaking_attention(q, k, v):
    bsz, seq = q.shape[:2]
    nb = seq // BLOCK
    kpos = jnp.arange(seq)

    def block(i):
        start = i * BLOCK
        qb = lax.dynamic_slice_in_dim(q, start, BLOCK, axis=1)
        qpos = start + jnp.arange(BLOCK)
        strict = (qpos[:, None] - kpos[None, :]) > 0
        z = jnp.einsum('bqhd,bkhd->bhqk', qb, k).astype(jnp.float32) * SCALE
        log_beta = jax.nn.log_sigmoid(z)
        log_1m_beta = jnp.where(strict, log_beta - z, 0.0)
        later = lax.cumsum(log_1m_beta, axis=3, reverse=True) - log_1m_beta
        a = jnp.where(strict, jnp.exp(log_beta + later), 0.0)
        return jnp.einsum('bhqk,bkhd->bqhd', a.astype(v.dtype), v)

    o = lax.map(block, jnp.arange(nb))
    return o.transpose(1, 0, 2, 3, 4).reshape(bsz, seq, C_WIDTH)


def conv_gated_ffn(h, w_up, w_conv, b_conv, w_down):
    seq = h.shape[1]
    u = h @ w_up
    up = jnp.pad(u, ((0, 0), (CONV_W - 1, 0), (0, 0)))
    u = sum(up[:, tap:tap + seq] * w_conv[tap] for tap in range(CONV_W)) + b_conv
    gate, val = jnp.split(u, 2, axis=-1)
    return (jax.nn.silu(gate) * val) @ w_down


def setup_inputs(seed: int = 0) -> dict:
    key = jax.random.key(seed)
    ks = jax.random.split(key, 22)

    def nrm(k, shape, scale):
        return jax.random.normal(k, shape, jnp.float32) * scale

    return {
        "x": nrm(ks[0], (BATCH, SEQ, D_MODEL), 1.0),
        "rel_bias": nrm(ks[1], (N_BUCKETS, SOFT_HEADS), 0.5),
        "norm_mix": 1.0 + nrm(ks[2], (DEPTH, D_MODEL), 0.02),
        "norm_ffn": 1.0 + nrm(ks[3], (DEPTH, D_MODEL), 0.02),
        "norm_final": 1.0 + nrm(ks[4], (D_MODEL,), 0.02),
        "w_in_even": nrm(ks[5], (N_EVEN, D_MODEL, EVEN_IN), D_MODEL ** -0.5),
        "w_out_even": nrm(ks[6], (N_EVEN, EVEN_OUT, D_MODEL), EVEN_OUT ** -0.5),
        "sinks": nrm(ks[7], (N_EVEN, A_HEADS), 0.5),
        "lam_q1": nrm(ks[8], (N_EVEN, HEAD_DIM), 0.1),
        "lam_k1": nrm(ks[9], (N_EVEN, HEAD_DIM), 0.1),
        "lam_q2": nrm(ks[10], (N_EVEN, HEAD_DIM), 0.1),
        "lam_k2": nrm(ks[11], (N_EVEN, HEAD_DIM), 0.1),
        "diff_norm": 1.0 + nrm(ks[12], (N_EVEN, B_VDIM), 0.02),
        "w_in_odd": nrm(ks[13], (N_ODD, D_MODEL, ODD_IN), D_MODEL ** -0.5),
        "w_out_odd": nrm(ks[14], (N_ODD, C_WIDTH, D_MODEL), C_WIDTH ** -0.5),
        "ffn_up": nrm(ks[15], (DEPTH, D_MODEL, 2 * D_FF), D_MODEL ** -0.5),
        "ffn_conv": nrm(ks[16], (DEPTH, CONV_W, 2 * D_FF), CONV_W ** -0.5),
        "ffn_conv_b": nrm(ks[17], (DEPTH, 2 * D_FF), 0.02),
        "ffn_down": nrm(ks[18], (DEPTH, D_FF, D_MODEL), D_FF ** -0.5),
    }


def reference(x, rel_bias, norm_mix, norm_ffn, norm_final, w_in_even, w_out_even, sinks,
              lam_q1, lam_k1, lam_q2, lam_k2, diff_norm, w_in_odd, w_out_odd,
              ffn_up, ffn_conv, ffn_conv_b, ffn_down):
    bsz, seq = x.shape[:2]
    for layer in range(DEPTH):
        h = rms_norm(x, norm_mix[layer])
        if layer % 2 == 0:
            e = layer // 2
            proj = h @ w_in_even[e]
            aq, ak, av, bq, bk, bv = jnp.split(proj, EVEN_SPLITS, axis=-1)
            aq = aq.reshape(bsz, seq, A_HEADS, HEAD_DIM)
            ak = ak.reshape(bsz, seq, A_KV_HEADS, HEAD_DIM)
            av = av.reshape(bsz, seq, A_KV_HEADS, HEAD_DIM)
            oa = sliding_window_sink_attention(aq, ak, av, sinks[e], rel_bias[:, :A_HEADS])
            bq = bq.reshape(bsz, seq, B_HEADS, 2, HEAD_DIM)
            bk = bk.reshape(bsz, seq, B_HEADS, 2, HEAD_DIM)
            bv = bv.reshape(bsz, seq, B_HEADS, B_VDIM)
            lambda_init = 0.8 - 0.6 * math.exp(-0.3 * layer)
            lam = (jnp.exp(jnp.sum(lam_q1[e].astype(jnp.float32) * lam_k1[e].astype(jnp.float32)))
                   - jnp.exp(jnp.sum(lam_q2[e].astype(jnp.float32) * lam_k2[e].astype(jnp.float32)))
                   + lambda_init)
            ob = differential_attention(bq[..., 0, :], bq[..., 1, :], bk[..., 0, :], bk[..., 1, :], bv,
                                        lam, lambda_init, rel_bias[:, A_HEADS:], diff_norm[e])
            x = x + jnp.concatenate([oa, ob], axis=-1) @ w_out_even[e]
        else:
            o = layer // 2
            cq, ck, cv = jnp.split(h @ w_in_odd[o], 3, axis=-1)
            shp = (bsz, seq, C_HEADS, HEAD_DIM)
            oc = stick_breaking_attention(cq.reshape(shp), ck.reshape(shp), cv.reshape(shp))
            x = x + oc @ w_out_odd[o]
        h = rms_norm(x, norm_ffn[layer])
        x = x + conv_gated_ffn(h, ffn_up[layer], ffn_conv[layer], ffn_conv_b[layer], ffn_down[layer])
    return rms_norm(x, norm_final)
```

```python
import math
from contextlib import ExitStack
import numpy as np
import concourse.bass as bass
import concourse.mybir as mybir
from concourse.bass_utils import run_bass_kernel_spmd

F32 = mybir.dt.float32
BF16 = mybir.dt.bfloat16
AF = mybir.ActivationFunctionType
ALU = mybir.AluOpType
AX = mybir.AxisListType

N_CORES = 8
D = 1024
DFF = 2816
EPS = 1e-6
NEG = -30000.0
N_DMA_SEMS = 48
N_KICK = 0

C_NMIX = 0
C_NFFN = 32
C_NFIN = 64
C_CONVW = 72
C_CONVB = 600
C_SINK = 776
C_DNORM = 792
C_LAM = 794
C_FARB = 1306
C_END = 1310
M_NEGTRI = 0
M_NEGONE = 128
M_ONE = 256
M_ONE_D = 384
M_ONE_128 = 512
M_SBMASK = 640
M_END = 1536


class Buf:
    __slots__ = ("name", "w", "r")

    def __init__(self, name=""):
        self.name = name
        self.w = None
        self.r = {}


class KB:
    def __init__(self, nc, stack):
        self.nc = nc
        self.engs = {"pe": nc.tensor, "act": nc.scalar, "dve": nc.vector,
                     "pool": nc.gpsimd, "sp": nc.sync}
        self.sems = {}
        for e in ("pe", "act", "dve", "pool"):
            self.sems[e] = stack.enter_context(nc.semaphore("s_" + e))
        self.cnt = {e: 0 for e in ("pe", "act", "dve", "pool")}
        self.dsems = [stack.enter_context(nc.semaphore("d%d" % i)) for i in range(N_DMA_SEMS)]
        self.dcnt = [0] * N_DMA_SEMS
        self.dnext = 0
        self.waited = {e: {} for e in self.engs}
        self.n_wait = 0
        self.n_ins = 0

    def _sem(self, key):
        return self.sems[key] if isinstance(key, str) else self.dsems[key]

    def _need(self, eng, ev, same_ok):
        if ev is None:
            return
        key, val = ev
        if key == eng and same_ok:
            return
        if self.waited[eng].get(key, 0) >= val:
            return
        self.engs[eng].wait_ge(self._sem(key), val)
        self.waited[eng][key] = val
        self.n_wait += 1

    def _deps(self, eng, reads, writes):
        for b in reads:
            self._need(eng, b.w, same_ok=False)
        so = (eng == "pe")
        for b in writes:
            self._need(eng, b.w, same_ok=so)
            for key, val in b.r.items():
                self._need(eng, (key, val), same_ok=so)

    def _mark(self, ev, reads, writes):
        key, val = ev
        for b in reads:
            if b.r.get(key, 0) < val:
                b.r[key] = val
        for b in writes:
            b.w = ev
            b.r = {}

    def op(self, eng, fn, reads=(), writes=()):
        self._deps(eng, reads, writes)
        ins = fn()
        self.cnt[eng] += 1
        ins.then_inc(self.sems[eng], 1)
        self._mark((eng, self.cnt[eng]), reads, writes)
        self.n_ins += 1
        return ins

    def dma(self, q, out, in_, reads=(), writes=()):
        k = self.dnext
        self.dnext = (self.dnext + 1) % N_DMA_SEMS
        if self.dcnt[k] > 0:
            self._need(q, (k, 16 * self.dcnt[k]), same_ok=False)
        self._deps(q, reads, writes)
        ins = self.engs[q].dma_start(out=out, in_=in_)
        self.dcnt[k] += 1
        ins.then_inc(self.dsems[k], 16)
        self._mark((k, 16 * self.dcnt[k]), reads, writes)
        self.n_ins += 1

    def barrier(self):
        for e in self.engs:
            for f in ("pe", "act", "dve", "pool"):
                if f != e and self.cnt[f] > 0:
                    self._need(e, (f, self.cnt[f]), same_ok=False)
            for k in range(N_DMA_SEMS):
                if self.dcnt[k] > 0:
                    self._need(e, (k, 16 * self.dcnt[k]), same_ok=False)


DEBUG = False
DBG = {}


def build_program(S, NSEQ, DEPTH):
    nc = bass.Bass("TRN2", target_bir_lowering=False)
    NT = S // 512
    NBLK = S // 128

    def din(name, shape):
        return nc.dram_tensor(name, shape, F32, kind="ExternalInput").ap()

    xT = din("xT", [NSEQ, D, S])
    w_in_even = din("w_in_even", [2, D, 2304])
    w_out_even = din("w_out_even", [2, D, D])
    w_in_odd = din("w_in_odd", [2, D, 3072])
    w_out_odd = din("w_out_odd", [2, D, D])
    ffn_up = din("ffn_up", [4, D, 2 * DFF])
    ffn_down = din("ffn_down", [4, DFF, D])
    cf32_d = din("cf32", [128, C_END])
    cmat_d = din("cmat", [128, M_END])
    biasB_d = din("biasB", [128, 4, 1024])
    negB_d = din("negB", [128, 1024])
    biasA_d = din("biasA", [128, 8, 512])
    negA_d = din("negA", [128, 512])
    yT = nc.dram_tensor("yT", [NSEQ, D, S], F32, kind="ExternalOutput").ap()

    sk = "ExternalOutput" if DEBUG else "Internal"
    xs = nc.dram_tensor("xs", [NSEQ, D, S], F32, kind=sk).ap()
    projT = nc.dram_tensor("projT", [NSEQ, 2048, S], BF16, kind=sk).ap()
    projV = nc.dram_tensor("projV", [NSEQ, S, 1024], BF16, kind=sk).ap()
    oT = nc.dram_tensor("oT", [NSEQ, D, S], BF16, kind=sk).ap()

    b_xs = [[Buf() for _ in range(S // 256)] for _ in range(NSEQ)]
    b_projT = [[Buf() for _ in range(NT)] for _ in range(NSEQ)]
    b_projV = [[Buf() for _ in range(NT)] for _ in range(NSEQ)]
    b_oT = [[Buf() for _ in range(NT)] for _ in range(NSEQ)]
    b_y = Buf()

    with ExitStack() as gs:
        kb = KB(nc, gs)

        uid = [0]

        def sb(stack, name, shape, dt):
            uid[0] += 1
            return stack.enter_context(nc.sbuf_tensor("%s_%d" % (name, uid[0]), shape, dt))

        psw = [gs.enter_context(nc.psum_tensor("psw%d" % i, [128, 1024], F32)) for i in range(4)]
        ps = [psw[i // 2][:, (i % 2) * 512:(i % 2 + 1) * 512] for i in range(8)]
        b_ps = [Buf() for _ in range(8)]

        cf = sb(gs, "cf", [128, C_END], F32)
        b_cf = Buf()
        cm = sb(gs, "cm", [128, M_END], BF16)
        b_cm = Buf()
        kb.dma("sp", cf[:], cf32_d[:, :], writes=[b_cf])
        with ExitStack() as st0:
            cmf = sb(st0, "cmf", [128, M_END], F32)
            b_cmf = Buf()
            kb.dma("sp", cmf[:], cmat_d[:, :], writes=[b_cmf])
            kb.op("dve", lambda: nc.vector.tensor_copy(cm[:], cmf[:]), reads=[b_cmf], writes=[b_cm])
            kb.barrier()
        negtri = cm[:, M_NEGTRI:M_NEGTRI + 128]
        negone = cm[:, M_NEGONE:M_NEGONE + 128]
        one_m = cm[:, M_ONE:M_ONE + 128]
        one_d = cm[:, M_ONE_D:M_ONE_D + 128]
        one_128 = cm[:, M_ONE_128:M_ONE_128 + 128]

        def cfc(col):
            return cf[:, col:col + 1]

        cvt_rr = [0]

        def convert(out_ap, in_ap, reads, writes, scale=None, engines=("dve", "act", "dve", "act", "pool")):
            e = engines[cvt_rr[0] % len(engines)]
            cvt_rr[0] += 1
            if scale is None:
                if e == "dve":
                    kb.op("dve", lambda: nc.vector.tensor_copy(out_ap, in_ap), reads=reads, writes=writes)
                elif e == "pool":
                    kb.op("pool", lambda: nc.gpsimd.tensor_copy(out_ap, in_ap), reads=reads, writes=writes)
                else:
                    kb.op("act", lambda: nc.scalar.copy(out_ap, in_ap), reads=reads, writes=writes)
            else:
                if e == "act":
                    kb.op("act", lambda: nc.scalar.mul(out_ap, in_ap, scale), reads=reads, writes=writes)
                elif e == "pool":
                    kb.op("pool", lambda: nc.gpsimd.tensor_scalar(out_ap, in_ap, scale, None, ALU.mult),
                          reads=reads, writes=writes)
                else:
                    kb.op("dve", lambda: nc.vector.tensor_scalar(out_ap, in_ap, scale, None, ALU.mult),
                          reads=reads, writes=writes)

        def fill_weight(wb, bw, w_ap, nrows, ncols, stg, bst, CW, scale_cols=(), q="sp",
                        engines=("dve", "act", "dve", "act", "pool")):
            nk = nrows // 128
            i = 0
            for k in range(nk):
                for c0 in range(0, ncols, CW):
                    s = i % 2
                    i += 1
                    kb.dma(q, stg[s][:, 0:CW], w_ap[k * 128:(k + 1) * 128, c0:c0 + CW], writes=[bst[s]])
                    segs = []
                    pos = c0
                    for (a, b_, sc) in scale_cols:
                        lo, hi = max(a, c0), min(b_, c0 + CW)
                        if lo < hi:
                            if pos < lo:
                                segs.append((pos, lo, None))
                            segs.append((lo, hi, sc))
                            pos = hi
                    if pos < c0 + CW:
                        segs.append((pos, c0 + CW, None))
                    for (a, b_, sc) in segs:
                        convert(wb[:, k, a:b_], stg[s][:, a - c0:b_ - c0], [bst[s]], [bw], sc, engines)

        def load_weight(st, name, w_ap, nrows, ncols, scale_cols=()):
            nk = nrows // 128
            wb = sb(st, name, [128, nk, ncols], BF16)
            bw = Buf()
            CW = 1408 if ncols % 1408 == 0 else (1536 if ncols % 1536 == 0 else 1024)
            if ncols % CW != 0:
                CW = ncols
            st_in = ExitStack()
            stg = [sb(st_in, name + "_s%d" % i, [128, CW], F32) for i in range(2)]
            bst = [Buf(), Buf()]
            fill_weight(wb, bw, w_ap, nrows, ncols, stg, bst, CW, scale_cols)
            kb.barrier()
            st_in.close()
            return wb, bw

        def rsqrt_op(out_ap, in_ap, b_in, b_out):
            kb.op("act", lambda: nc.scalar.activation(out=out_ap, in_=in_ap, func=AF.Ln, bias=EPS),
                  reads=[b_in], writes=[b_out])
            kb.op("act", lambda: nc.scalar.activation(out=out_ap, in_=out_ap, func=AF.Exp, scale=-0.5),
                  reads=[b_out], writes=[b_out])

        def rmsnorm_tile(xin, b_xin, hT, b_hT, sq, b_sq, rstd, b_rstd, psb, gcol0, ntok):
            for c in range(8):
                s = c % 2
                kb.op("act", lambda: nc.scalar.activation(out=sq[s][:, 0:ntok], in_=xin[:, c, :], func=AF.Square),
                      reads=[b_xin], writes=[b_sq[s]])
                kb.op("pe", lambda: nc.tensor.matmul(ps[psb][:, 0:ntok], lhsT=one_d, rhs=sq[s][:, 0:ntok],
                                                      start=(c == 0), stop=(c == 7)),
                      reads=[b_sq[s], b_cm], writes=[b_ps[psb]])
            rsqrt_op(rstd[:, 0:ntok], ps[psb][:, 0:ntok], b_ps[psb], b_rstd)
            for c in range(8):
                kb.op("dve", lambda: nc.vector.scalar_tensor_tensor(
                    out=hT[:, c, :], in0=xin[:, c, :], scalar=cfc(gcol0 + c), in1=rstd[:, 0:ntok],
                    op0=ALU.mult, op1=ALU.mult), reads=[b_xin, b_rstd, b_cf], writes=[b_hT])

        evac_rr = [0]

        def evac(out_ap, in_ap, reads, writes):
            e = ("act", "dve")[evac_rr[0] % 2]
            evac_rr[0] += 1
            if e == "act":
                kb.op("act", lambda: nc.scalar.copy(out_ap, in_ap), reads=reads, writes=writes)
            else:
                kb.op("dve", lambda: nc.vector.tensor_copy(out_ap, in_ap), reads=reads, writes=writes)

        def phase_inproj(layer, xsrc, b_xsrc_fn):
            even = (layer % 2 == 0)
            if even:
                w_ap, ncols = w_in_even[layer // 2], 2304
                fm = [(c * 128, c * 128) for c in range(4)] + [(512, 512)] + \
                     [(768 + c * 128, 640 + c * 128) for c in range(4)] + \
                     [(1280 + c * 128, 1152 + c * 128) for c in range(4)]
                tm = [(640, 128, 0), (1792, 512, 128)]
                scale_cols = [(0, 512, 0.125), (768, 1280, 0.125)]
            else:
                w_ap, ncols = w_in_odd[layer // 2], 3072
                fm = [(c * 128, c * 128) for c in range(16)]
                tm = [(2048, 512, 0), (2560, 512, 512)]
                scale_cols = [(0, 1024, 0.125)]
            with ExitStack() as st:
                wb, bw = load_weight(st, "win", w_ap, D, ncols, scale_cols)
                xin = [sb(st, "xin%d" % i, [128, 8, 512], F32) for i in range(2)]
                b_xin = [Buf(), Buf()]
                hT = [sb(st, "hT%d" % i, [128, 8, 512], BF16) for i in range(2)]
                b_hT = [Buf(), Buf()]
                sq = [sb(st, "sq%d" % i, [128, 512], BF16) for i in range(2)]
                b_sq = [Buf(), Buf()]
                rstd = sb(st, "rstd", [128, 512], F32)
                b_rstd = Buf()
                NST = 4
                stg = [sb(st, "ostg%d" % i, [128, 512], BF16) for i in range(NST)]
                b_stg = [Buf() for _ in range(NST)]
                tiles = [(s, t) for s in range(NSEQ) for t in range(NT)]

                def load_x(i):
                    s, t = tiles[i]
                    src = xsrc[s].rearrange("(c p) n -> p c n", p=128)[:, :, t * 512:(t + 1) * 512]
                    kb.dma("sp", xin[i % 2][:], src, reads=b_xsrc_fn(s, t), writes=[b_xin[i % 2]])

                def norm(i):
                    rmsnorm_tile(xin[i % 2], b_xin[i % 2], hT[i % 2], b_hT[i % 2], sq, b_sq, rstd, b_rstd, 7,
                                 C_NMIX + layer * 8, 512)

                load_x(0)
                if len(tiles) > 1:
                    load_x(1)
                norm(0)
                si = 0
                pb = 0
                for i, (s, t) in enumerate(tiles):
                    if i + 1 < len(tiles):
                        norm(i + 1)
                    if i + 2 < len(tiles):
                        load_x(i + 2)
                    h_, bh_ = hT[i % 2], b_hT[i % 2]
                    for (wc, dr) in fm:
                        p = pb % 6
                        pb += 1
                        for k in range(8):
                            kb.op("pe", lambda: nc.tensor.matmul(ps[p][:], lhsT=wb[:, k, wc:wc + 128], rhs=h_[:, k, :],
                                                                  start=(k == 0), stop=(k == 7)),
                                  reads=[bw, bh_], writes=[b_ps[p]])
                        g = si % NST
                        si += 1
                        evac(stg[g][:], ps[p][:], [b_ps[p]], [b_stg[g]])
                        kb.dma("sp", projT[s, dr:dr + 128, t * 512:(t + 1) * 512], stg[g][:],
                               reads=[b_stg[g]], writes=[b_projT[s][t]])
                    for tb in range(4):
                        for (wc, wd, dc) in tm:
                            p = pb % 6
                            pb += 1
                            for k in range(8):
                                kb.op("pe", lambda: nc.tensor.matmul(ps[p][:, 0:wd], lhsT=h_[:, k, tb * 128:(tb + 1) * 128],
                                                                      rhs=wb[:, k, wc:wc + wd], start=(k == 0), stop=(k == 7)),
                                      reads=[bw, bh_], writes=[b_ps[p]])
                            g = si % NST
                            si += 1
                            evac(stg[g][:, 0:wd], ps[p][:, 0:wd], [b_ps[p]], [b_stg[g]])
                            r0 = t * 512 + tb * 128
                            kb.dma("sp", projV[s, r0:r0 + 128, dc:dc + wd], stg[g][:, 0:wd],
                                   reads=[b_stg[g]], writes=[b_projV[s][t]])
                kb.barrier()

        def phase_stickbreak():
            with ExitStack() as st:
                vall = sb(st, "vall", [128, NBLK, 1024], BF16)
                b_v = Buf()
                qh = [sb(st, "qh%d" % i, [64, S], BF16) for i in range(2)]
                kh = [sb(st, "kh%d" % i, [64, S], BF16) for i in range(2)]
                b_qk = [Buf(), Buf()]
                e_t = [sb(st, "e%d" % i, [128, 1024], F32) for i in range(2)]
                b_e = [Buf(), Buf()]
                sp_t = [sb(st, "sp%d" % i, [128, 1024], BF16) for i in range(2)]
                b_sp = [Buf(), Buf()]
                a_t = [sb(st, "a%d" % i, [128, 1024], BF16) for i in range(3)]
                b_a = [Buf() for _ in range(3)]
                R_t = [sb(st, "R%d" % i, [128, 512], BF16) for i in range(2)]
                b_R = [Buf(), Buf()]
                og = [sb(st, "og%d" % i, [64, 512], BF16) for i in range(2)]
                b_og = [Buf(), Buf()]
                items = []
                hidx = 0
                gcount = -1
                for s in range(NSEQ):
                    for h in range(16):
                        for qt in range(NT):
                            nb = 4 * qt + 4
                            gcount += 1
                            rpar = 0
                            for n in range(nb // 2):
                                kbA = nb - 1 - 2 * n
                                items.append(dict(s=s, h=h, hidx=hidx, qt=qt, kbA=kbA, kbB=kbA - 1, first=(n == 0),
                                                  last=(n == nb // 2 - 1), jA=(kbA - 4 * qt) if kbA >= 4 * qt else None,
                                                  newhead=(qt == 0 and n == 0), newseq=(h == 0 and qt == 0 and n == 0),
                                                  g=gcount, rcur=rpar))
                                rpar ^= 1
                        hidx += 1
                n = len(items)
                allT = lambda s: b_projT[s]
                allV = lambda s: b_projV[s]

                def load_head(s, h, hidx):
                    sl = hidx % 2
                    kb.dma("sp", qh[sl][:], projT[s, h * 64:(h + 1) * 64, :], reads=allT(s), writes=[b_qk[sl]])
                    kb.dma("sp", kh[sl][:], projT[s, 1024 + h * 64:1024 + (h + 1) * 64, :], reads=allT(s), writes=[b_qk[sl]])

                def mask_ap(j):
                    c0 = M_SBMASK + 384 - 128 * j
                    return cm[:, c0:c0 + 512]

                def stA(i):
                    it = items[i]
                    s, h, qt = it["s"], it["h"], it["qt"]
                    sl = it["hidx"] % 2
                    if it["newseq"]:
                        vsrc = projV[s].rearrange("(b p) d -> p b d", p=128)
                        step = max(1, NBLK // 4)
                        for b0 in range(0, NBLK, step):
                            kb.dma("sp", vall[:, b0:b0 + step, :], vsrc[:, b0:b0 + step, :], reads=allV(s), writes=[b_v])
                    if it["newhead"]:
                        if i == 0:
                            load_head(s, h, it["hidx"])
                        nh = it["hidx"] + 1
                        if nh < NSEQ * 16:
                            load_head(nh // 16, nh % 16, nh)
                    w = i % 3
                    bw2 = [b_ps[2 * w], b_ps[2 * w + 1]]
                    for half, kbk in ((0, it["kbA"]), (1, it["kbB"])):
                        kb.op("pe", lambda: nc.tensor.matmul(ps[2 * w + half][:], lhsT=kh[sl][:, kbk * 128:(kbk + 1) * 128],
                                                              rhs=qh[sl][:, qt * 512:(qt + 1) * 512], start=True, stop=False),
                              reads=[b_qk[sl]], writes=[b_ps[2 * w + half]])
                    kb.op("act", lambda: nc.scalar.activation(out=e_t[i % 2][:], in_=psw[w][:], func=AF.Exp),
                          reads=bw2, writes=[b_e[i % 2]])
                    kb.op("act", lambda: nc.scalar.activation(out=sp_t[i % 2][:], in_=e_t[i % 2][:], func=AF.Ln, bias=1.0),
                          reads=[b_e[i % 2]], writes=[b_sp[i % 2]])
                    if it["jA"] is not None:
                        for half, j in ((0, it["jA"]), (1, it["jA"] - 1)):
                            kb.op("dve", lambda: nc.vector.tensor_tensor(out=sp_t[i % 2][:, half * 512:(half + 1) * 512],
                                                                          in0=sp_t[i % 2][:, half * 512:(half + 1) * 512],
                                                                          in1=mask_ap(j), op=ALU.mult),
                                  reads=[b_sp[i % 2], b_cm], writes=[b_sp[i % 2]])

                def stB(i):
                    it = items[i]
                    w = i % 3
                    rc = it["rcur"]
                    first = it["first"]
                    spA = sp_t[i % 2][:, 0:512]
                    spB = sp_t[i % 2][:, 512:1024]
                    bA, bB = 2 * w, 2 * w + 1
                    kb.op("pe", lambda: nc.tensor.matmul(ps[bA][:], lhsT=negtri, rhs=spA, start=False, stop=first),
                          reads=[b_sp[i % 2], b_cm], writes=[b_ps[bA]])
                    if not first:
                        kb.op("pe", lambda: nc.tensor.matmul(ps[bA][:], lhsT=negone, rhs=R_t[rc][:], start=False, stop=True),
                              reads=[b_R[rc], b_cm], writes=[b_ps[bA]])
                    kb.op("pe", lambda: nc.tensor.matmul(ps[bB][:], lhsT=negtri, rhs=spB, start=False, stop=False),
                          reads=[b_sp[i % 2], b_cm], writes=[b_ps[bB]])
                    kb.op("pe", lambda: nc.tensor.matmul(ps[bB][:], lhsT=negone, rhs=spA, start=False, stop=first),
                          reads=[b_sp[i % 2], b_cm], writes=[b_ps[bB]])
                    if not first:
                        kb.op("pe", lambda: nc.tensor.matmul(ps[bB][:], lhsT=negone, rhs=R_t[rc][:], start=False, stop=True),
                              reads=[b_R[rc], b_cm], writes=[b_ps[bB]])
                    if not it["last"]:
                        if first:
                            kb.op("dve", lambda: nc.vector.tensor_tensor(out=R_t[rc ^ 1][:], in0=spA, in1=spB, op=ALU.add),
                                  reads=[b_sp[i % 2]], writes=[b_R[rc ^ 1]])
                        else:
                            kb.op("dve", lambda: nc.vector.tensor_tensor(out=R_t[rc ^ 1][:], in0=R_t[rc][:], in1=spA, op=ALU.add),
                                  reads=[b_sp[i % 2], b_R[rc]], writes=[b_R[rc ^ 1]])
                            kb.op("dve", lambda: nc.vector.tensor_tensor(out=R_t[rc ^ 1][:], in0=R_t[rc ^ 1][:], in1=spB, op=ALU.add),
                                  reads=[b_sp[i % 2], b_R[rc ^ 1]], writes=[b_R[rc ^ 1]])
                    kb.op("act", lambda: nc.scalar.activation(out=a_t[i % 3][:], in_=psw[w][:], func=AF.Exp),
                          reads=[b_ps[bA], b_ps[bB]], writes=[b_a[i % 3]])
                    if it["jA"] is not None:
                        for half, j in ((0, it["jA"]), (1, it["jA"] - 1)):
                            kb.op("dve", lambda: nc.vector.tensor_tensor(out=a_t[i % 3][:, half * 512:(half + 1) * 512],
                                                                          in0=a_t[i % 3][:, half * 512:(half + 1) * 512],
                                                                          in1=mask_ap(j), op=ALU.mult),
                                  reads=[b_a[i % 3], b_cm], writes=[b_a[i % 3]])

                def stC(i):
                    it = items[i]
                    s, h, qt = it["s"], it["h"], it["qt"]
                    ob = 6 + (it["g"] % 2)
                    for half, kbk in ((0, it["kbA"]), (1, it["kbB"])):
                        kb.op("pe", lambda: nc.tensor.matmul(ps[ob][0:64, :], lhsT=vall[:, kbk, h * 64:(h + 1) * 64],
                                                              rhs=a_t[i % 3][:, half * 512:(half + 1) * 512],
                                                              start=(it["first"] and half == 0), stop=(it["last"] and half == 1)),
                              reads=[b_v, b_a[i % 3]], writes=[b_ps[ob]])
                    if it["last"]:
                        g = it["g"] % 2
                        kb.op("dve", lambda: nc.vector.tensor_copy(og[g][:], ps[ob][0:64, :]), reads=[b_ps[ob]], writes=[b_og[g]])
                        kb.dma("sp", oT[s, h * 64:(h + 1) * 64, qt * 512:(qt + 1) * 512], og[g][:],
                               reads=[b_og[g]], writes=[b_oT[s][qt]])

                for step in range(n + 2):
                    if step < n:
                        stA(step)
                    if 0 <= step - 1 < n:
                        stB(step - 1)
                    if 0 <= step - 2 < n:
                        stC(step - 2)
                kb.barrier()

        def phase_even_attn(layer):
            e = layer // 2
            lam_init = 0.8 - 0.6 * math.exp(-0.3 * layer)
            with ExitStack() as st:
                TB = sb(st, "TB", [128, 4, 1024], F32)
                b_TB = Buf()
                TA = sb(st, "TA", [128, 8, 512], F32)
                b_TA = Buf()
                ngB = sb(st, "ngB", [128, 1024], F32)
                ngA = sb(st, "ngA", [128, 512], F32)
                b_ng = Buf()
                sm = sb(st, "sm", [128, 32], F32)
                b_sm = Buf()
                lt = sb(st, "lt", [128, 2, 64], F32)
                b_lt = Buf()
                kb.dma("sp", TB[:], biasB_d[:, :, :], writes=[b_TB])
                kb.dma("sp", TA[:], biasA_d[:, :, :], writes=[b_TA])
                kb.dma("sp", ngB[:], negB_d[:, :], writes=[b_ng])
                kb.dma("sp", ngA[:], negA_d[:, :], writes=[b_ng])
                for h in range(4):
                    kb.op("dve", lambda: nc.vector.tensor_tensor(out=TB[:, h, :], in0=TB[:, h, :], in1=ngB[:], op=ALU.add),
                          reads=[b_TB, b_ng], writes=[b_TB])
                for h in range(8):
                    kb.op("dve", lambda: nc.vector.tensor_tensor(out=TA[:, h, :], in0=TA[:, h, :], in1=ngA[:], op=ALU.add),
                          reads=[b_TA, b_ng], writes=[b_TA])
                kb.op("act", lambda: nc.scalar.activation(out=sm[:, 0:8], in_=cf[:, C_SINK + e * 8:C_SINK + e * 8 + 8], func=AF.Exp),
                      reads=[b_cf], writes=[b_sm])
                lq = cf[:, C_LAM + e * 256:C_LAM + e * 256 + 256].rearrange("p (a d) -> p a d", a=4)
                kb.op("dve", lambda: nc.vector.tensor_tensor(out=lt[:, 0, :], in0=lq[:, 0, :], in1=lq[:, 1, :], op=ALU.mult),
                      reads=[b_cf], writes=[b_lt])
                kb.op("dve", lambda: nc.vector.tensor_tensor(out=lt[:, 1, :], in0=lq[:, 2, :], in1=lq[:, 3, :], op=ALU.mult),
                      reads=[b_cf], writes=[b_lt])
                kb.op("dve", lambda: nc.vector.tensor_reduce(out=sm[:, 8:10], in_=lt[:], axis=AX.X, op=ALU.add),
                      reads=[b_lt], writes=[b_sm])
                kb.op("act", lambda: nc.scalar.activation(out=sm[:, 12:14], in_=sm[:, 8:10], func=AF.Exp),
                      reads=[b_sm], writes=[b_sm])
                kb.op("dve", lambda: nc.vector.scalar_tensor_tensor(out=sm[:, 10:11], in0=sm[:, 13:14], scalar=-lam_init,
                                                                     in1=sm[:, 12:13], op0=ALU.add, op1=ALU.subtract),
                      reads=[b_sm], writes=[b_sm])
                kb.op("dve", lambda: nc.vector.tensor_scalar(sm[:, 11:12], cf[:, C_DNORM + e:C_DNORM + e + 1],
                                                              1.0 - lam_init, None, ALU.mult),
                      reads=[b_cf, b_sm], writes=[b_sm])

                avall = sb(st, "avall", [128, NBLK, 128], BF16)
                b_av = Buf()
                bvall = sb(st, "bvall", [128, NBLK, 512], BF16)
                b_bv = Buf()
                qh = [sb(st, "qh%d" % i, [64, S], BF16) for i in range(2)]
                b_q = [Buf(), Buf()]
                kh = [sb(st, "kh%d" % i, [64, S], BF16) for i in range(2)]
                b_k = [Buf(), Buf()]
                tmp = [sb(st, "tmp%d" % i, [128, 512], F32) for i in range(2)]
                b_tmp = [Buf(), Buf()]
                p_t = [sb(st, "p%d" % i, [128, 512], BF16) for i in range(3)]
                b_p = [Buf() for _ in range(3)]
                dtmp = [sb(st, "dtmp%d" % i, [128, 512], F32) for i in range(3)]
                b_dtmp = [Buf() for _ in range(3)]
                dp = [sb(st, "dp%d" % i, [128, 512], BF16) for i in range(4)]
                b_dp = [Buf() for _ in range(4)]
                og = [sb(st, "og%d" % i, [128, 512], BF16) for i in range(2)]
                b_og = [Buf(), Buf()]
                kh2 = sb(st, "kh2", [128, S], BF16)
                b_kh2 = Buf()
                qz = [sb(st, "qz%d" % i, [128, S], BF16) for i in range(2)]
                b_qz = [Buf(), Buf()]
                kb.op("pool", lambda: nc.gpsimd.memset(qz[0][64:128, :], 0.0), writes=[b_qz[0]])
                kb.op("pool", lambda: nc.gpsimd.memset(qz[1][0:64, :], 0.0), writes=[b_qz[1]])
                swf = [sb(st, "swf%d" % i, [64, 256], F32) for i in range(2)]
                b_swf = [Buf(), Buf()]
                dacc = [[sb(st, "dacc%d_%d" % (k, r), [128, 512], F32) for r in range(2)] for k in range(2)]
                b_dacc = [[Buf(), Buf()], [Buf(), Buf()]]
                dacp = [[sb(st, "dacp%d_%d" % (k, r), [128, 512], F32) for r in range(2)] for k in range(2)]
                b_dacp = [[Buf(), Buf()], [Buf(), Buf()]]
                daccb = [[sb(st, "daccb%d_%d" % (k, r), [128, 512], BF16) for r in range(2)] for k in range(2)]
                b_daccb = [[Buf(), Buf()], [Buf(), Buf()]]
                fin2 = [[sb(st, "fin2_%d_%d" % (k, i), [128, 512], F32) for i in range(4)] for k in range(2)]
                b_fin2 = [[Buf() for _ in range(4)] for _ in range(2)]
                sqd2 = [sb(st, "sqd2_%d" % k, [128, 512], BF16) for k in range(2)]
                b_sqd2 = [Buf(), Buf()]

                qi = 0
                ki = 0
                it_i = 0
                for s in range(NSEQ):
                    vsrc = projV[s].rearrange("(b p) d -> p b d", p=128)
                    kb.dma("sp", avall[:], vsrc[:, :, 0:128], reads=b_projV[s], writes=[b_av])
                    step = max(1, NBLK // 4)
                    for b0 in range(0, NBLK, step):
                        kb.dma("sp", bvall[:, b0:b0 + step, :], vsrc[:, b0:b0 + step, 128:640], reads=b_projV[s], writes=[b_bv])
                    sitems = [(g, hh, qp) for g in range(2) for hh in range(4) for qp in range(NBLK // 2)]
                    ns_ = len(sitems)
                    sw_slots = {}

                    def sA(i):
                        g, hh, qp = sitems[i]
                        h = g * 4 + hh
                        if hh == 0 and qp == 0:
                            kb.dma("sp", kh[g][:], projT[s, 512 + g * 64:512 + (g + 1) * 64, :], reads=b_projT[s], writes=[b_k[g]])
                        if qp == 0:
                            kb.dma("sp", qh[h % 2][:], projT[s, h * 64:(h + 1) * 64, :], reads=b_projT[s], writes=[b_q[h % 2]])
                        qs, ks = h % 2, g
                        zb, tp, pp = i % 2, i % 2, i % 3
                        for u in range(2):
                            qb = 2 * qp + u
                            qsl = qh[qs][:, qb * 128:(qb + 1) * 128]
                            if qb > 0:
                                kb.op("pe", lambda: nc.tensor.matmul(ps[zb][:, u * 256:u * 256 + 128],
                                                                      lhsT=kh[ks][:, (qb - 1) * 128:qb * 128], rhs=qsl,
                                                                      start=True, stop=True),
                                      reads=[b_k[ks], b_q[qs]], writes=[b_ps[zb]])
                            kb.op("pe", lambda: nc.tensor.matmul(ps[zb][:, u * 256 + 128:u * 256 + 256],
                                                                  lhsT=kh[ks][:, qb * 128:(qb + 1) * 128], rhs=qsl,
                                                                  start=True, stop=True),
                                  reads=[b_k[ks], b_q[qs]], writes=[b_ps[zb]])
                        c0 = 128 if qp == 0 else 0
                        kb.op("dve", lambda: nc.vector.tensor_tensor(out=tmp[tp][:, c0:512], in0=ps[zb][:, c0:512],
                                                                      in1=TA[:, h, c0:512], op=ALU.add),
                              reads=[b_ps[zb], b_TA], writes=[b_tmp[tp]])
                        kb.op("act", lambda: nc.scalar.activation(out=p_t[pp][:, c0:512], in_=tmp[tp][:, c0:512], func=AF.Exp),
                              reads=[b_tmp[tp]], writes=[b_p[pp]])

                    def sB(i):
                        g, hh, qp = sitems[i]
                        pp = i % 3
                        nb = 2 + i % 2
                        db = 4 + i % 2
                        for u in range(2):
                            qb = 2 * qp + u
                            first = True
                            for half, kblk in ((0, qb - 1), (1, qb)):
                                if kblk < 0:
                                    continue
                                pc = u * 256 + half * 128
                                lastm = (half == 1)
                                kb.op("pe", lambda: nc.tensor.matmul(ps[nb][0:64, u * 128:(u + 1) * 128],
                                                                      lhsT=avall[:, kblk, g * 64:(g + 1) * 64],
                                                                      rhs=p_t[pp][:, pc:pc + 128], start=first, stop=lastm),
                                      reads=[b_av, b_p[pp]], writes=[b_ps[nb]])
                                kb.op("pe", lambda: nc.tensor.matmul(ps[db][0:64, u * 128:(u + 1) * 128],
                                                                      lhsT=one_m[:, 0:64],
                                                                      rhs=p_t[pp][:, pc:pc + 128], start=first, stop=lastm),
                                      reads=[b_cm, b_p[pp]], writes=[b_ps[db]])
                                first = False

                    def sC1(i):
                        g, hh, qp = sitems[i]
                        h = g * 4 + hh
                        db = 4 + i % 2
                        f0 = swf[i % 2]
                        bf0 = b_swf[i % 2]
                        kb.op("act", lambda: nc.scalar.activation(out=f0[0:64, :], in_=ps[db][0:64, 0:256], func=AF.Ln,
                                                                   bias=sm[0:64, h:h + 1]),
                              reads=[b_ps[db], b_sm], writes=[bf0])
                        kb.op("act", lambda: nc.scalar.activation(out=f0[0:64, :], in_=f0[0:64, :], func=AF.Exp, scale=-1.0),
                              reads=[bf0], writes=[bf0])

                    def sC2(i):
                        g, hh, qp = sitems[i]
                        h = g * 4 + hh
                        nb = 2 + i % 2
                        f0 = swf[i % 2]
                        bf0 = b_swf[i % 2]
                        oslot = (qp // 2) % 2
                        ocol = (qp % 2) * 256
                        kb.op("dve", lambda: nc.vector.tensor_tensor(out=og[oslot][0:64, ocol:ocol + 256],
                                                                      in0=ps[nb][0:64, 0:256], in1=f0[0:64, :],
                                                                      op=ALU.mult),
                              reads=[b_ps[nb], bf0], writes=[b_og[oslot]])
                        if qp % 2 == 1:
                            qt = qp // 2
                            kb.dma("sp", oT[s, h * 64:(h + 1) * 64, qt * 512:(qt + 1) * 512], og[oslot][0:64, :],
                                   reads=[b_og[oslot]], writes=[b_oT[s][qt]])

                    for step in range(ns_ + 2):
                        if 0 <= step - 2 < ns_:
                            sC1(step - 2)
                        if step < ns_:
                            sA(step)
                        if 0 <= step - 1 < ns_:
                            sB(step - 1)
                        if 0 <= step - 2 < ns_:
                            sC2(step - 2)

                    ditems = []
                    gg = -1
                    for h in range(4):
                        for qt in range(NT):
                            gg += 1
                            nbk = 4 * qt + 4
                            for r in range(2):
                                for kbk in range(nbk):
                                    ditems.append((h, qt, r, kbk, nbk, gg))
                    nd = len(ditems)
                    deferred = []

                    def finalize(h, qt, g, step):
                        k = g % 2
                        dset = (6, 7)
                        fa = fin2[k]
                        bfa = b_fin2[k]

                        def F1():
                            for r in range(2):
                                kb.op("act", lambda: nc.scalar.activation(out=fa[2 + r][:], in_=ps[dset[r]][:], func=AF.Ln),
                                      reads=[b_ps[dset[r]]], writes=[bfa[2 + r]])

                        def F2():
                            for r in range(2):
                                kb.op("act", lambda: nc.scalar.activation(out=fa[2 + r][:], in_=fa[2 + r][:], func=AF.Exp, scale=-1.0),
                                      reads=[bfa[2 + r]], writes=[bfa[2 + r]])
                            for r in range(2):
                                kb.op("dve", lambda: nc.vector.tensor_tensor(out=fa[r][:], in0=fa[r][:], in1=fa[2 + r][:], op=ALU.mult),
                                      reads=[bfa[r], bfa[2 + r]], writes=[bfa[r]])
                            kb.op("dve", lambda: nc.vector.scalar_tensor_tensor(out=fa[2][:], in0=fa[1][:], scalar=sm[:, 10:11],
                                                                                 in1=fa[0][:], op0=ALU.mult, op1=ALU.add),
                                  reads=[bfa[0], bfa[1], bfa[2], b_sm], writes=[bfa[2]])
                            kb.op("act", lambda: nc.scalar.activation(out=sqd2[k][:], in_=fa[2][:], func=AF.Square),
                                  reads=[bfa[2]], writes=[b_sqd2[k]])

                        def F3():
                            kb.op("pe", lambda: nc.tensor.matmul(ps[2][:], lhsT=one_128, rhs=sqd2[k][:], start=True, stop=True),
                                  reads=[b_cm, b_sqd2[k]], writes=[b_ps[2]])

                        def F4():
                            rsqrt_op(fa[3][:], ps[2][:], b_ps[2], bfa[3])
                            kb.op("dve", lambda: nc.vector.scalar_tensor_tensor(out=og[k][:], in0=fa[2][:], scalar=sm[:, 11:12],
                                                                                 in1=fa[3][:], op0=ALU.mult, op1=ALU.mult),
                                  reads=[bfa[2], bfa[3], b_sm], writes=[b_og[k]])
                            kb.dma("sp", oT[s, 512 + h * 128:512 + (h + 1) * 128, qt * 512:(qt + 1) * 512], og[k][:],
                                   reads=[b_og[k]], writes=[b_oT[s][qt]])

                        F1()
                        deferred.append((step + 2, F2))
                        deferred.append((step + 4, F3))
                        deferred.append((step + 6, F4))

                    def dA(i):
                        h, qt, r, kbk, nbk, g = ditems[i]
                        if qt == 0 and r == 0 and kbk == 0:
                            row = h * 128
                            kb.dma("sp", kh2[:], projT[s, 1152 + row:1152 + row + 128, :], reads=b_projT[s], writes=[b_kh2])
                            kb.dma("sp", qz[0][0:64, :], projT[s, 640 + row:640 + row + 64, :], reads=b_projT[s], writes=[b_qz[0]])
                            kb.dma("sp", qz[1][64:128, :], projT[s, 640 + row + 64:640 + row + 128, :], reads=b_projT[s],
                                   writes=[b_qz[1]])
                        zb = (0, 1, 3)[i % 3]
                        tp = i % 3
                        pp = i % 4
                        if qt == 0 and r == 0 and kbk == 0:
                            for _ in range(N_KICK):
                                kb.op("pe", lambda: nc.tensor.matmul(ps[zb][:], lhsT=one_m, rhs=cm[:, 0:512], start=True, stop=True),
                                      reads=[b_cm], writes=[b_ps[zb]])
                        kb.op("pe", lambda: nc.tensor.matmul(ps[zb][:], lhsT=kh2[:, kbk * 128:(kbk + 1) * 128],
                                                              rhs=qz[r][:, qt * 512:(qt + 1) * 512], start=True, stop=True),
                              reads=[b_kh2, b_qz[r]], writes=[b_ps[zb]])
                        delta = qt * 512 - kbk * 128
                        if delta <= 128:
                            c0 = delta + 384
                            kb.op("dve", lambda: nc.vector.tensor_tensor(out=dtmp[tp][:], in0=ps[zb][:],
                                                                          in1=TB[:, h, c0:c0 + 512], op=ALU.add),
                                  reads=[b_ps[zb], b_TB], writes=[b_dtmp[tp]])
                            kb.op("act", lambda: nc.scalar.activation(out=dp[pp][:], in_=dtmp[tp][:], func=AF.Exp),
                                  reads=[b_dtmp[tp]], writes=[b_dp[pp]])
                        else:
                            kb.op("act", lambda: nc.scalar.activation(out=dp[pp][:], in_=ps[zb][:], func=AF.Exp,
                                                                       bias=cfc(C_FARB + h)),
                                  reads=[b_ps[zb], b_cf], writes=[b_dp[pp]])

                    def dB(i, step):
                        h, qt, r, kbk, nbk, g = ditems[i]
                        pp = i % 4
                        pvb = 4 + r
                        dnb = (6, 7)[r]
                        kb.op("pe", lambda: nc.tensor.matmul(ps[pvb][:], lhsT=bvall[:, kbk, h * 128:(h + 1) * 128],
                                                              rhs=dp[pp][:], start=(kbk == 0), stop=(kbk == nbk - 1)),
                              reads=[b_bv, b_dp[pp]], writes=[b_ps[pvb]])
                        k = g % 2
                        if kbk % 2 == 0:
                            if kbk == 0:
                                kb.op("dve", lambda: nc.vector.tensor_copy(dacc[k][r][:], dp[pp][:]),
                                      reads=[b_dp[pp]], writes=[b_dacc[k][r]])
                            else:
                                kb.op("dve", lambda: nc.vector.tensor_tensor(out=dacc[k][r][:], in0=dacc[k][r][:], in1=dp[pp][:],
                                                                              op=ALU.add),
                                      reads=[b_dp[pp], b_dacc[k][r]], writes=[b_dacc[k][r]])
                        else:
                            if kbk == 1:
                                kb.op("pool", lambda: nc.gpsimd.tensor_copy(dacp[k][r][:], dp[pp][:]),
                                      reads=[b_dp[pp]], writes=[b_dacp[k][r]])
                            else:
                                kb.op("pool", lambda: nc.gpsimd.tensor_tensor(out=dacp[k][r][:], in0=dacp[k][r][:], in1=dp[pp][:],
                                                                               op=ALU.add),
                                      reads=[b_dp[pp], b_dacp[k][r]], writes=[b_dacp[k][r]])
                        if kbk == nbk - 1:
                            kb.op("dve", lambda: nc.vector.tensor_tensor(out=daccb[k][r][:], in0=dacc[k][r][:], in1=dacp[k][r][:],
                                                                          op=ALU.add),
                                  reads=[b_dacc[k][r], b_dacp[k][r]], writes=[b_daccb[k][r]])

                            def den_mm(k=k, r=r, dnb=dnb):
                                kb.op("pe", lambda: nc.tensor.matmul(ps[dnb][:], lhsT=one_m, rhs=daccb[k][r][:], start=True, stop=True),
                                      reads=[b_cm, b_daccb[k][r]], writes=[b_ps[dnb]])
                            deferred.append((step + 1, den_mm))
                            if r == 1:
                                for rr in range(2):
                                    kb.op("dve", lambda: nc.vector.tensor_copy(fin2[k][rr][:], ps[4 + rr][:]),
                                          reads=[b_ps[4 + rr]], writes=[b_fin2[k][rr]])
                                deferred.append((step + 2, lambda: finalize(h, qt, g, step + 2)))

                    for step in range(nd + 2):
                        if step < nd:
                            dA(step)
                        if step >= 2:
                            dB(step - 2, step)
                        while True:
                            deferred.sort(key=lambda e: e[0])
                            if not (deferred and deferred[0][0] <= step):
                                break
                            deferred.pop(0)[1]()
                    while deferred:
                        deferred.sort(key=lambda e: e[0])
                        deferred.pop(0)[1]()
                kb.barrier()

        def phase_outproj(layer, xsrc, first_layer, mid_cb=None):
            w_ap = (w_out_even if layer % 2 == 0 else w_out_odd)[layer // 2]
            TP = 256
            with ExitStack() as st:
                wb, bw = load_weight(st, "wout", w_ap, D, D)
                if mid_cb is not None:
                    mid_cb()
                xin = [sb(st, "xin%d" % i, [128, 8, TP], F32) for i in range(2)]
                b_xin = [Buf(), Buf()]
                oin = [sb(st, "oin%d" % i, [128, 8, TP], BF16) for i in range(2)]
                b_oin = [Buf(), Buf()]
                tiles = [(s, t) for s in range(NSEQ) for t in range(S // TP)]

                def load(i):
                    s, t = tiles[i]
                    src = xsrc[s].rearrange("(c p) n -> p c n", p=128)[:, :, t * TP:(t + 1) * TP]
                    kb.dma("sp", xin[i % 2][:], src, reads=([] if first_layer else [b_xs[s][t]]), writes=[b_xin[i % 2]])
                    osrc = oT[s].rearrange("(c p) n -> p c n", p=128)[:, :, t * TP:(t + 1) * TP]
                    kb.dma("sp", oin[i % 2][:], osrc, reads=[b_oT[s][t // 2]], writes=[b_oin[i % 2]])

                load(0)
                pb = 0
                for i, (s, t) in enumerate(tiles):
                    if i + 1 < len(tiles):
                        load(i + 1)
                    x_, bx_ = xin[i % 2], b_xin[i % 2]
                    o_, bo_ = oin[i % 2], b_oin[i % 2]
                    for oc in range(8):
                        p = pb % 8
                        pb += 1
                        for k in range(8):
                            kb.op("pe", lambda: nc.tensor.matmul(ps[p][:, 0:TP], lhsT=wb[:, k, oc * 128:(oc + 1) * 128], rhs=o_[:, k, :],
                                                                  start=(k == 0), stop=(k == 7)),
                                  reads=[bw, bo_], writes=[b_ps[p]])
                        kb.op("dve", lambda: nc.vector.tensor_tensor(out=x_[:, oc, :], in0=x_[:, oc, :], in1=ps[p][:, 0:TP], op=ALU.add),
                              reads=[b_ps[p], bx_], writes=[bx_])
                    dst = xs[s].rearrange("(c p) n -> p c n", p=128)[:, :, t * TP:(t + 1) * TP]
                    kb.dma("sp", dst, x_[:], reads=[bx_], writes=[b_xs[s][t]])
                kb.barrier()

        def phase_ffn(layer, final, wts):
            TF = 256
            NTF = S // TF
            wu, bwu, wd, bwd = wts
            with ExitStack() as st:
                xin = [sb(st, "xin%d" % i, [128, 8, TF], F32) for i in range(3)]
                b_xin = [Buf() for _ in range(3)]
                hT2 = [sb(st, "hT%d" % i, [128, 8, TF], BF16) for i in range(2)]
                b_hT2 = [Buf(), Buf()]
                sq = [sb(st, "sq%d" % i, [128, TF], BF16) for i in range(2)]
                b_sq = [Buf(), Buf()]
                rstd = sb(st, "rstd", [128, TF], F32)
                b_rstd = Buf()
                G = [sb(st, "G%d" % i, [128, 22, TF], BF16) for i in range(2)]
                b_G = [Buf(), Buf()]
                carry = sb(st, "carry", [128, 22, 2, 2], F32)
                b_carry = [Buf() for _ in range(22)]
                ub = [sb(st, "ub%d" % i, [128, 2, TF + 2], F32) for i in range(2)]
                b_ub = [Buf(), Buf()]
                b_ubc = [Buf(), Buf()]
                cg = [sb(st, "cg%d" % i, [128, TF], F32) for i in range(2)]
                b_cg = [Buf(), Buf()]
                cv = [sb(st, "cv%d" % i, [128, TF], F32) for i in range(2)]
                b_cv = [Buf(), Buf()]
                sg = [sb(st, "sg%d" % i, [128, TF], F32) for i in range(2)]
                b_sg = [Buf(), Buf()]
                tiles = [(s, t) for s in range(NSEQ) for t in range(NTF)]

                def load(i):
                    s, t = tiles[i]
                    src = xs[s].rearrange("(c p) n -> p c n", p=128)[:, :, t * TF:(t + 1) * TF]
                    kb.dma("sp", xin[i % 3][:], src, reads=[b_xs[s][t]], writes=[b_xin[i % 3]])

                def cw(tap, ch):
                    return cfc(C_CONVW + layer * 132 + tap * 44 + ch)

                def cb(ch):
                    return cfc(C_CONVB + layer * 44 + ch)

                def down_work(i):
                    s, t = tiles[i]
                    x_, bx_ = xin[i % 3], b_xin[i % 3]
                    G_, bG_ = G[i % 2], b_G[i % 2]
                    work = []
                    for oc in range(8):
                        p = 3 + oc % 4
                        for k in range(22):
                            def mm(oc=oc, k=k, p=p):
                                kb.op("pe", lambda: nc.tensor.matmul(ps[p][:, 0:TF], lhsT=wd[:, k, oc * 128:(oc + 1) * 128],
                                                                      rhs=G_[:, k, :], start=(k == 0), stop=(k == 21)),
                                      reads=[bwd, bG_], writes=[b_ps[p]])
                                if k == 21:
                                    kb.op("dve", lambda: nc.vector.tensor_tensor(out=x_[:, oc, :], in0=x_[:, oc, :],
                                                                                  in1=ps[p][:, 0:TF], op=ALU.add),
                                          reads=[b_ps[p], bx_], writes=[bx_])
                            work.append(mm)

                    def post():
                        if not final:
                            dst = xs[s].rearrange("(c p) n -> p c n", p=128)[:, :, t * TF:(t + 1) * TF]
                            kb.dma("sp", dst, x_[:], reads=[bx_], writes=[b_xs[s][t]])
                        else:
                            for c in range(8):
                                q = c % 2
                                kb.op("act", lambda: nc.scalar.activation(out=sq[q][:], in_=x_[:, c, :], func=AF.Square),
                                      reads=[bx_], writes=[b_sq[q]])
                                kb.op("pe", lambda: nc.tensor.matmul(ps[7][:, 0:TF], lhsT=one_d, rhs=sq[q][:],
                                                                      start=(c == 0), stop=(c == 7)),
                                      reads=[b_sq[q], b_cm], writes=[b_ps[7]])
                            rsqrt_op(rstd[:], ps[7][:, 0:TF], b_ps[7], b_rstd)
                            for c in range(8):
                                kb.op("dve", lambda: nc.vector.scalar_tensor_tensor(
                                    out=x_[:, c, :], in0=x_[:, c, :], scalar=cfc(C_NFIN + c), in1=rstd[:],
                                    op0=ALU.mult, op1=ALU.mult), reads=[bx_, b_rstd, b_cf], writes=[bx_])
                            dst = yT[s].rearrange("(c p) n -> p c n", p=128)[:, :, t * TF:(t + 1) * TF]
                            kb.dma("sp", dst, x_[:], reads=[bx_], writes=[b_y])
                    return work, post

                load(0)
                pb = 0
                ui = 0
                pending, pending_post = [], None

                def drain(nmax):
                    nonlocal pending, pending_post
                    for _ in range(min(nmax, len(pending))):
                        pending.pop(0)()
                    if not pending and pending_post is not None:
                        pending_post()
                        pending_post = None

                for i, (s, t) in enumerate(tiles):
                    if i + 1 < len(tiles):
                        load(i + 1)
                    x_, bx_ = xin[i % 3], b_xin[i % 3]
                    G_, bG_ = G[i % 2], b_G[i % 2]
                    if t == 0:
                        kb.op("pool", lambda: nc.gpsimd.memset(carry[:], 0.0), writes=b_carry)
                    drain(22)
                    hT, b_hT = hT2[i % 2], b_hT2[i % 2]
                    if i == 0:
                        rmsnorm_tile(x_, bx_, hT, b_hT, sq, b_sq, rstd, b_rstd, 7, C_NFFN + layer * 8, TF)
                    for c in range(22):
                        if c == 11 and i + 1 < len(tiles):
                            rmsnorm_tile(xin[(i + 1) % 3], b_xin[(i + 1) % 3], hT2[(i + 1) % 2], b_hT2[(i + 1) % 2],
                                         sq, b_sq, rstd, b_rstd, 7, C_NFFN + layer * 8, TF)
                        p = pb % 3
                        pb += 1
                        u_ = ui % 2
                        ui += 1
                        for half, col in ((0, c * 128), (1, DFF + c * 128)):
                            for k in range(8):
                                kb.op("pe", lambda: nc.tensor.matmul(ps[p][:, half * TF:(half + 1) * TF],
                                                                      lhsT=wu[:, k, col:col + 128], rhs=hT[:, k, :],
                                                                      start=(k == 0), stop=(k == 7)),
                                      reads=[bwu, b_hT], writes=[b_ps[p]])
                        drain(7)
                        U = ub[u_]
                        bU = b_ub[u_]
                        bUc = b_ubc[u_]
                        kb.op("pool", lambda: nc.gpsimd.tensor_copy(U[:, :, 0:2], carry[:, c, :, :]),
                              reads=[b_carry[c]], writes=[bUc])
                        kb.op("act", lambda: nc.scalar.copy(U[:, :, 2:TF + 2], ps[p][:, :].rearrange("p (a n) -> p a n", a=2)),
                              reads=[b_ps[p]], writes=[bU])
                        kb.op("pool", lambda: nc.gpsimd.tensor_copy(carry[:, c, :, :], U[:, :, TF:TF + 2]),
                              reads=[bU], writes=[b_carry[c]])
                        for half, ch, dst, bdst in ((0, c, cg[u_], b_cg[u_]), (1, 22 + c, cv[u_], b_cv[u_])):
                            kb.op("act", lambda: nc.scalar.activation(out=dst[:], in_=ps[p][:, half * TF:(half + 1) * TF],
                                                                       func=AF.Identity, scale=cw(2, ch), bias=cb(ch)),
                                  reads=[b_ps[p], b_cf], writes=[bdst])
                            kb.op("dve", lambda: nc.vector.scalar_tensor_tensor(out=dst[:], in0=U[:, half, 1:TF + 1], scalar=cw(1, ch),
                                                                                 in1=dst[:], op0=ALU.mult, op1=ALU.add),
                                  reads=[bU, bUc, bdst, b_cf], writes=[bdst])
                            kb.op("dve", lambda: nc.vector.scalar_tensor_tensor(out=dst[:], in0=U[:, half, 0:TF], scalar=cw(0, ch),
                                                                                 in1=dst[:], op0=ALU.mult, op1=ALU.add),
                                  reads=[bU, bUc, bdst, b_cf], writes=[bdst])
                        kb.op("act", lambda: nc.scalar.activation(out=sg[u_][:], in_=cg[u_][:], func=AF.Silu),
                              reads=[b_cg[u_]], writes=[b_sg[u_]])
                        kb.op("dve", lambda: nc.vector.tensor_tensor(out=G_[:, c, :], in0=sg[u_][:], in1=cv[u_][:], op=ALU.mult),
                              reads=[b_sg[u_], b_cv[u_]], writes=[bG_])
                    drain(10 ** 6)
                    pending, pending_post = down_work(i)
                drain(10 ** 6)
                kb.barrier()

        for layer in range(DEPTH):
            if layer == 0:
                xsrc, bfn = xT, (lambda s, t: [])
            else:
                xsrc, bfn = xs, (lambda s, t: [b_xs[s][2 * t], b_xs[s][2 * t + 1]])
            phase_inproj(layer, xsrc, bfn)
            if layer % 2 == 0:
                phase_even_attn(layer)
            else:
                phase_stickbreak()
            with ExitStack() as fst:
                wu = sb(fst, "wup", [128, 8, 2 * DFF], BF16)
                wd = sb(fst, "wdn", [128, 22, D], BF16)
                bwu, bwd = Buf(), Buf()
                with ExitStack() as sst:
                    pstg = [sb(sst, "pf_s%d" % i, [128, 1408], F32) for i in range(2)]
                    pbst = [Buf(), Buf()]

                    def prefetch():
                        fill_weight(wu, bwu, ffn_up[layer], D, 2 * DFF, pstg, pbst, 1408, q="act", engines=("act", "pool", "act"))
                        fill_weight(wd, bwd, ffn_down[layer], DFF, D, pstg, pbst, 1024, q="act", engines=("act", "pool", "act"))

                    phase_outproj(layer, xsrc, layer == 0, prefetch)
                phase_ffn(layer, (layer == DEPTH - 1), (wu, bwu, wd, bwd))
        kb.barrier()
        print("program: %d instructions, %d waits" % (kb.n_ins, kb.n_wait))
    return nc


def _t5_bucket_np(d):
    n = np.maximum(d, 0)
    nf = np.maximum(n, 1).astype(np.float32)
    large = 16 + (np.log(nf / np.float32(16)) / np.float32(math.log(128 / 16)) * np.float32(16)).astype(np.int32)
    large = np.minimum(large, 31)
    return np.where(n < 16, n, large)


def _static_tables():
    p = np.arange(128)[:, None]
    cmat = np.zeros((128, M_END), np.float32)
    j = np.arange(128)[None, :]
    cmat[:, M_NEGTRI:M_NEGTRI + 128] = -(p >= j).astype(np.float32)
    cmat[:, M_NEGONE:M_NEGONE + 128] = -1.0
    cmat[:, M_ONE:M_ONE + 128] = 1.0
    cmat[:, M_ONE_D:M_ONE_D + 128] = 1.0 / 1024
    cmat[:, M_ONE_128:M_ONE_128 + 128] = 1.0 / 128
    x = np.arange(896)[None, :]
    cmat[:, M_SBMASK:M_SBMASK + 896] = ((x - 384 - p) > 0).astype(np.float32)
    jB = np.arange(1024)[None, :]
    dB = jB - 384 - p
    negB = np.where(dB < 0, NEG, 0.0).astype(np.float32)
    idxB = _t5_bucket_np(dB)
    c = np.arange(128)[None, :]
    d0 = 128 + c - p
    d1 = c - p
    dA = np.concatenate([d0, d1, d0, d1], axis=1)
    negA = np.where((dA < 0) | (dA >= 128), NEG, 0.0).astype(np.float32)
    idxA = _t5_bucket_np(np.clip(dA, 0, 127))
    return cmat, negB, idxB, negA, idxA


def _prep_shared(inp):
    cmat, negB, idxB, negA, idxA = _static_tables()
    rb = np.asarray(inp["rel_bias"], np.float32)
    biasB = np.ascontiguousarray(rb[idxB][:, :, 8:12].transpose(0, 2, 1))
    biasA = np.ascontiguousarray(rb[idxA][:, :, 0:8].transpose(0, 2, 1))
    cf = np.zeros((128, C_END), np.float32)

    def fm(v):
        v = np.asarray(v, np.float32)
        return np.moveaxis(v.reshape(v.shape[:-1] + (8, 128)), -1, 0)

    cf[:, C_NMIX:C_NMIX + 32] = fm(inp["norm_mix"]).reshape(128, 32)
    cf[:, C_NFFN:C_NFFN + 32] = fm(inp["norm_ffn"]).reshape(128, 32)
    cf[:, C_NFIN:C_NFIN + 8] = fm(inp["norm_final"]).reshape(128, 8)
    cw = np.asarray(inp["ffn_conv"], np.float32).reshape(4, 3, 44, 128)
    cf[:, C_CONVW:C_CONVW + 528] = np.moveaxis(cw, -1, 0).reshape(128, 528)
    cbv = np.asarray(inp["ffn_conv_b"], np.float32).reshape(4, 44, 128)
    cf[:, C_CONVB:C_CONVB + 176] = np.moveaxis(cbv, -1, 0).reshape(128, 176)
    cf[:, C_SINK:C_SINK + 16] = np.broadcast_to(np.asarray(inp["sinks"], np.float32).reshape(1, 16), (128, 16))
    cf[:, C_DNORM:C_DNORM + 2] = np.asarray(inp["diff_norm"], np.float32).T
    lam = np.stack([np.asarray(inp[k], np.float32) for k in ("lam_q1", "lam_k1", "lam_q2", "lam_k2")], axis=1)
    cf[:, C_LAM:C_LAM + 512] = np.broadcast_to(lam.reshape(1, 512), (128, 512))
    cf[:, C_FARB:C_FARB + 4] = np.broadcast_to(rb[31, 8:12].reshape(1, 4), (128, 4))
    shared = {
        "w_in_even": np.ascontiguousarray(inp["w_in_even"], np.float32),
        "w_out_even": np.ascontiguousarray(inp["w_out_even"], np.float32),
        "w_in_odd": np.ascontiguousarray(inp["w_in_odd"], np.float32),
        "w_out_odd": np.ascontiguousarray(inp["w_out_odd"], np.float32),
        "ffn_up": np.ascontiguousarray(inp["ffn_up"], np.float32),
        "ffn_down": np.ascontiguousarray(inp["ffn_down"], np.float32),
        "cf32": cf, "cmat": cmat, "biasB": biasB, "negB": negB, "biasA": biasA, "negA": negA,
    }
    return shared


def run(inp, S, NSEQ, DEPTH, n_cores):
    x = np.asarray(inp["x"], np.float32)
    shared = _prep_shared(inp)
    nc = build_program(S, NSEQ, DEPTH)
    in_maps = []
    for c in range(n_cores):
        m = dict(shared)
        m["xT"] = np.ascontiguousarray(x[c * NSEQ:(c + 1) * NSEQ].transpose(0, 2, 1))
        in_maps.append(m)
    res = run_bass_kernel_spmd(nc, in_maps, core_ids=list(range(n_cores)))
    if DEBUG:
        for k in ("xs", "projT", "projV", "oT"):
            DBG[k] = [np.asarray(r[k]) for r in res.results]
    outs = [np.asarray(r["yT"]).transpose(0, 2, 1) for r in res.results]
    return np.ascontiguousarray(np.concatenate(outs, axis=0)).astype(np.float32)


def kernel(**inputs):
    return run(inputs, S=4096, NSEQ=2, DEPTH=4, n_cores=N_CORES)
```

```python
import math
from contextlib import ExitStack
import numpy as np
import concourse.bass as bass
import concourse.mybir as mybir
from concourse.bass_utils import run_bass_kernel_spmd

F32 = mybir.dt.float32
BF16 = mybir.dt.bfloat16
AF = mybir.ActivationFunctionType
ALU = mybir.AluOpType
AX = mybir.AxisListType

N_CORES = 8
D = 1024
DFF = 2816
EPS = 1e-6
NEG = -30000.0
N_DMA_SEMS = 48
N_KICK = 0

C_NMIX = 0
C_NFFN = 32
C_NFIN = 64
C_CONVW = 72
C_CONVB = 600
C_SINK = 776
C_DNORM = 792
C_LAM = 794
C_FARB = 1306
C_END = 1310
M_NEGTRI = 0
M_NEGONE = 128
M_ONE = 256
M_ONE_D = 384
M_ONE_128 = 512
M_SBMASK = 640
M_END = 1536


class Buf:
    __slots__ = ("name", "w", "r")

    def __init__(self, name=""):
        self.name = name
        self.w = None
        self.r = {}


class KB:
    def __init__(self, nc, stack):
        self.nc = nc
        self.engs = {"pe": nc.tensor, "act": nc.scalar, "dve": nc.vector,
                     "pool": nc.gpsimd, "sp": nc.sync}
        self.sems = {}
        for e in ("pe", "act", "dve", "pool"):
            self.sems[e] = stack.enter_context(nc.semaphore("s_" + e))
        self.cnt = {e: 0 for e in ("pe", "act", "dve", "pool")}
        self.dsems = [stack.enter_context(nc.semaphore("d%d" % i)) for i in range(N_DMA_SEMS)]
        self.dcnt = [0] * N_DMA_SEMS
        self.dnext = 0
        self.waited = {e: {} for e in self.engs}
        self.n_wait = 0
        self.n_ins = 0

    def _sem(self, key):
        return self.sems[key] if isinstance(key, str) else self.dsems[key]

    def _need(self, eng, ev, same_ok):
        if ev is None:
            return
        key, val = ev
        if key == eng and same_ok:
            return
        if self.waited[eng].get(key, 0) >= val:
            return
        self.engs[eng].wait_ge(self._sem(key), val)
        self.waited[eng][key] = val
        self.n_wait += 1

    def _deps(self, eng, reads, writes):
        for b in reads:
            self._need(eng, b.w, same_ok=False)
        for b in writes:
            self._need(eng, b.w, same_ok=True)
            for key, val in b.r.items():
                self._need(eng, (key, val), same_ok=True)

    def _mark(self, ev, reads, writes):
        key, val = ev
        for b in reads:
            if b.r.get(key, 0) < val:
                b.r[key] = val
        for b in writes:
            b.w = ev
            b.r = {}

    def op(self, eng, fn, reads=(), writes=()):
        self._deps(eng, reads, writes)
        ins = fn()
        self.cnt[eng] += 1
        ins.then_inc(self.sems[eng], 1)
        self._mark((eng, self.cnt[eng]), reads, writes)
        self.n_ins += 1
        return ins

    def dma(self, q, out, in_, reads=(), writes=()):
        k = self.dnext
        self.dnext = (self.dnext + 1) % N_DMA_SEMS
        if self.dcnt[k] > 0:
            self._need(q, (k, 16 * self.dcnt[k]), same_ok=False)
        self._deps(q, reads, writes)
        ins = self.engs[q].dma_start(out=out, in_=in_)
        self.dcnt[k] += 1
        ins.then_inc(self.dsems[k], 16)
        self._mark((k, 16 * self.dcnt[k]), reads, writes)
        self.n_ins += 1

    def barrier(self):
        for e in self.engs:
            for f in ("pe", "act", "dve", "pool"):
                if f != e and self.cnt[f] > 0:
                    self._need(e, (f, self.cnt[f]), same_ok=False)
            for k in range(N_DMA_SEMS):
                if self.dcnt[k] > 0:
                    self._need(e, (k, 16 * self.dcnt[k]), same_ok=False)


DEBUG = False
DBG = {}


def build_program(S, NSEQ, DEPTH):
    nc = bass.Bass("TRN2", target_bir_lowering=False)
    NT = S // 512
    NBLK = S // 128

    def din(name, shape):
        return nc.dram_tensor(name, shape, F32, kind="ExternalInput").ap()

    xT = din("xT", [NSEQ, D, S])
    w_in_even = din("w_in_even", [2, D, 2304])
    w_out_even = din("w_out_even", [2, D, D])
    w_in_odd = din("w_in_odd", [2, D, 3072])
    w_out_odd = din("w_out_odd", [2, D, D])
    ffn_up = din("ffn_up", [4, D, 2 * DFF])
    ffn_down = din("ffn_down", [4, DFF, D])
    cf32_d = din("cf32", [128, C_END])
    cmat_d = din("cmat", [128, M_END])
    biasB_d = din("biasB", [128, 4, 1024])
    negB_d = din("negB", [128, 1024])
    biasA_d = din("biasA", [128, 8, 512])
    negA_d = din("negA", [128, 512])
    yT = nc.dram_tensor("yT", [NSEQ, D, S], F32, kind="ExternalOutput").ap()

    sk = "ExternalOutput" if DEBUG else "Internal"
    xs = nc.dram_tensor("xs", [NSEQ, D, S], F32, kind=sk).ap()
    projT = nc.dram_tensor("projT", [NSEQ, 2048, S], BF16, kind=sk).ap()
    projV = nc.dram_tensor("projV", [NSEQ, S, 1024], BF16, kind=sk).ap()
    oT = nc.dram_tensor("oT", [NSEQ, D, S], BF16, kind=sk).ap()

    b_xs = [[Buf() for _ in range(S // 256)] for _ in range(NSEQ)]
    b_projT = [[Buf() for _ in range(NT)] for _ in range(NSEQ)]
    b_projV = [[Buf() for _ in range(NT)] for _ in range(NSEQ)]
    b_oT = [[Buf() for _ in range(NT)] for _ in range(NSEQ)]
    b_y = Buf()

    with ExitStack() as gs:
        kb = KB(nc, gs)

        uid = [0]

        def sb(stack, name, shape, dt):
            uid[0] += 1
            return stack.enter_context(nc.sbuf_tensor("%s_%d" % (name, uid[0]), shape, dt))

        psw = [gs.enter_context(nc.psum_tensor("psw%d" % i, [128, 1024], F32)) for i in range(4)]
        ps = [psw[i // 2][:, (i % 2) * 512:(i % 2 + 1) * 512] for i in range(8)]
        b_ps = [Buf() for _ in range(8)]

        cf = sb(gs, "cf", [128, C_END], F32)
        b_cf = Buf()
        cm = sb(gs, "cm", [128, M_END], BF16)
        b_cm = Buf()
        kb.dma("sp", cf[:], cf32_d[:, :], writes=[b_cf])
        with ExitStack() as st0:
            cmf = sb(st0, "cmf", [128, M_END], F32)
            b_cmf = Buf()
            kb.dma("sp", cmf[:], cmat_d[:, :], writes=[b_cmf])
            kb.op("dve", lambda: nc.vector.tensor_copy(cm[:], cmf[:]), reads=[b_cmf], writes=[b_cm])
            kb.barrier()
        negtri = cm[:, M_NEGTRI:M_NEGTRI + 128]
        negone = cm[:, M_NEGONE:M_NEGONE + 128]
        one_m = cm[:, M_ONE:M_ONE + 128]
        one_d = cm[:, M_ONE_D:M_ONE_D + 128]
        one_128 = cm[:, M_ONE_128:M_ONE_128 + 128]

        def cfc(col):
            return cf[:, col:col + 1]

        cvt_rr = [0]

        def convert(out_ap, in_ap, reads, writes, scale=None, engines=("dve", "act", "dve", "act", "pool")):
            e = engines[cvt_rr[0] % len(engines)]
            cvt_rr[0] += 1
            if scale is None:
                if e == "dve":
                    kb.op("dve", lambda: nc.vector.tensor_copy(out_ap, in_ap), reads=reads, writes=writes)
                elif e == "pool":
                    kb.op("pool", lambda: nc.gpsimd.tensor_copy(out_ap, in_ap), reads=reads, writes=writes)
                else:
                    kb.op("act", lambda: nc.scalar.copy(out_ap, in_ap), reads=reads, writes=writes)
            else:
                if e == "act":
                    kb.op("act", lambda: nc.scalar.mul(out_ap, in_ap, scale), reads=reads, writes=writes)
                elif e == "pool":
                    kb.op("pool", lambda: nc.gpsimd.tensor_scalar(out_ap, in_ap, scale, None, ALU.mult),
                          reads=reads, writes=writes)
                else:
                    kb.op("dve", lambda: nc.vector.tensor_scalar(out_ap, in_ap, scale, None, ALU.mult),
                          reads=reads, writes=writes)

        def fill_weight(wb, bw, w_ap, nrows, ncols, stg, bst, CW, scale_cols=(), q="sp",
                        engines=("dve", "act", "dve", "act", "pool")):
            nk = nrows // 128
            i = 0
            for k in range(nk):
                for c0 in range(0, ncols, CW):
                    s = i % 2
                    i += 1
                    kb.dma(q, stg[s][:, 0:CW], w_ap[k * 128:(k + 1) * 128, c0:c0 + CW], writes=[bst[s]])
                    segs = []
                    pos = c0
                    for (a, b_, sc) in scale_cols:
                        lo, hi = max(a, c0), min(b_, c0 + CW)
                        if lo < hi:
                            if pos < lo:
                                segs.append((pos, lo, None))
                            segs.append((lo, hi, sc))
                            pos = hi
                    if pos < c0 + CW:
                        segs.append((pos, c0 + CW, None))
                    for (a, b_, sc) in segs:
                        convert(wb[:, k, a:b_], stg[s][:, a - c0:b_ - c0], [bst[s]], [bw], sc, engines)

        def load_weight(st, name, w_ap, nrows, ncols, scale_cols=()):
            nk = nrows // 128
            wb = sb(st, name, [128, nk, ncols], BF16)
            bw = Buf()
            CW = 1408 if ncols % 1408 == 0 else (1536 if ncols % 1536 == 0 else 1024)
            if ncols % CW != 0:
                CW = ncols
            st_in = ExitStack()
            stg = [sb(st_in, name + "_s%d" % i, [128, CW], F32) for i in range(2)]
            bst = [Buf(), Buf()]
            fill_weight(wb, bw, w_ap, nrows, ncols, stg, bst, CW, scale_cols)
            kb.barrier()
            st_in.close()
            return wb, bw

        def rsqrt_op(out_ap, in_ap, b_in, b_out):
            kb.op("act", lambda: nc.scalar.activation(out=out_ap, in_=in_ap, func=AF.Ln, bias=EPS),
                  reads=[b_in], writes=[b_out])
            kb.op("act", lambda: nc.scalar.activation(out=out_ap, in_=out_ap, func=AF.Exp, scale=-0.5),
                  reads=[b_out], writes=[b_out])

        def rmsnorm_tile(xin, b_xin, hT, b_hT, sq, b_sq, rstd, b_rstd, psb, gcol0, ntok):
            for c in range(8):
                s = c % 2
                kb.op("act", lambda: nc.scalar.activation(out=sq[s][:, 0:ntok], in_=xin[:, c, :], func=AF.Square),
                      reads=[b_xin], writes=[b_sq[s]])
                kb.op("pe", lambda: nc.tensor.matmul(ps[psb][:, 0:ntok], lhsT=one_d, rhs=sq[s][:, 0:ntok],
                                                      start=(c == 0), stop=(c == 7)),
                      reads=[b_sq[s], b_cm], writes=[b_ps[psb]])
            rsqrt_op(rstd[:, 0:ntok], ps[psb][:, 0:ntok], b_ps[psb], b_rstd)
            for c in range(8):
                kb.op("dve", lambda: nc.vector.scalar_tensor_tensor(
                    out=hT[:, c, :], in0=xin[:, c, :], scalar=cfc(gcol0 + c), in1=rstd[:, 0:ntok],
                    op0=ALU.mult, op1=ALU.mult), reads=[b_xin, b_rstd, b_cf], writes=[b_hT])

        evac_rr = [0]

        def evac(out_ap, in_ap, reads, writes):
            e = ("act", "dve")[evac_rr[0] % 2]
            evac_rr[0] += 1
            if e == "act":
                kb.op("act", lambda: nc.scalar.copy(out_ap, in_ap), reads=reads, writes=writes)
            else:
                kb.op("dve", lambda: nc.vector.tensor_copy(out_ap, in_ap), reads=reads, writes=writes)

        def phase_inproj(layer, xsrc, b_xsrc_fn):
            even = (layer % 2 == 0)
            if even:
                w_ap, ncols = w_in_even[layer // 2], 2304
                fm = [(c * 128, c * 128) for c in range(4)] + [(512, 512)] + \
                     [(768 + c * 128, 640 + c * 128) for c in range(4)] + \
                     [(1280 + c * 128, 1152 + c * 128) for c in range(4)]
                tm = [(640, 128, 0), (1792, 512, 128)]
                scale_cols = [(0, 512, 0.125), (768, 1280, 0.125)]
            else:
                w_ap, ncols = w_in_odd[layer // 2], 3072
                fm = [(c * 128, c * 128) for c in range(16)]
                tm = [(2048, 512, 0), (2560, 512, 512)]
                scale_cols = [(0, 1024, 0.125)]
            with ExitStack() as st:
                wb, bw = load_weight(st, "win", w_ap, D, ncols, scale_cols)
                xin = [sb(st, "xin%d" % i, [128, 8, 512], F32) for i in range(2)]
                b_xin = [Buf(), Buf()]
                hT = [sb(st, "hT%d" % i, [128, 8, 512], BF16) for i in range(2)]
                b_hT = [Buf(), Buf()]
                sq = [sb(st, "sq%d" % i, [128, 512], BF16) for i in range(2)]
                b_sq = [Buf(), Buf()]
                rstd = sb(st, "rstd", [128, 512], F32)
                b_rstd = Buf()
                NST = 4
                stg = [sb(st, "ostg%d" % i, [128, 512], BF16) for i in range(NST)]
                b_stg = [Buf() for _ in range(NST)]
                tiles = [(s, t) for s in range(NSEQ) for t in range(NT)]

                def load_x(i):
                    s, t = tiles[i]
                    src = xsrc[s].rearrange("(c p) n -> p c n", p=128)[:, :, t * 512:(t + 1) * 512]
                    kb.dma("sp", xin[i % 2][:], src, reads=b_xsrc_fn(s, t), writes=[b_xin[i % 2]])

                def norm(i):
                    rmsnorm_tile(xin[i % 2], b_xin[i % 2], hT[i % 2], b_hT[i % 2], sq, b_sq, rstd, b_rstd, 7,
                                 C_NMIX + layer * 8, 512)

                load_x(0)
                if len(tiles) > 1:
                    load_x(1)
                norm(0)
                si = 0
                pb = 0
                for i, (s, t) in enumerate(tiles):
                    if i + 1 < len(tiles):
                        norm(i + 1)
                    if i + 2 < len(tiles):
                        load_x(i + 2)
                    h_, bh_ = hT[i % 2], b_hT[i % 2]
                    for (wc, dr) in fm:
                        p = pb % 6
                        pb += 1
                        for k in range(8):
                            kb.op("pe", lambda: nc.tensor.matmul(ps[p][:], lhsT=wb[:, k, wc:wc + 128], rhs=h_[:, k, :],
                                                                  start=(k == 0), stop=(k == 7)),
                                  reads=[bw, bh_], writes=[b_ps[p]])
                        g = si % NST
                        si += 1
                        evac(stg[g][:], ps[p][:], [b_ps[p]], [b_stg[g]])
                        kb.dma("sp", projT[s, dr:dr + 128, t * 512:(t + 1) * 512], stg[g][:],
                               reads=[b_stg[g]], writes=[b_projT[s][t]])
                    for tb in range(4):
                        for (wc, wd, dc) in tm:
                            p = pb % 6
                            pb += 1
                            for k in range(8):
                                kb.op("pe", lambda: nc.tensor.matmul(ps[p][:, 0:wd], lhsT=h_[:, k, tb * 128:(tb + 1) * 128],
                                                                      rhs=wb[:, k, wc:wc + wd], start=(k == 0), stop=(k == 7)),
                                      reads=[bw, bh_], writes=[b_ps[p]])
                            g = si % NST
                            si += 1
                            evac(stg[g][:, 0:wd], ps[p][:, 0:wd], [b_ps[p]], [b_stg[g]])
                            r0 = t * 512 + tb * 128
                            kb.dma("sp", projV[s, r0:r0 + 128, dc:dc + wd], stg[g][:, 0:wd],
                                   reads=[b_stg[g]], writes=[b_projV[s][t]])
                kb.barrier()

        def phase_stickbreak():
            with ExitStack() as st:
                vall = sb(st, "vall", [128, NBLK, 1024], BF16)
                b_v = Buf()
                qh = [sb(st, "qh%d" % i, [64, S], BF16) for i in range(2)]
                kh = [sb(st, "kh%d" % i, [64, S], BF16) for i in range(2)]
                b_qk = [Buf(), Buf()]
                e_t = [sb(st, "e%d" % i, [128, 1024], F32) for i in range(2)]
                b_e = [Buf(), Buf()]
                sp_t = [sb(st, "sp%d" % i, [128, 1024], BF16) for i in range(2)]
                b_sp = [Buf(), Buf()]
                a_t = [sb(st, "a%d" % i, [128, 1024], BF16) for i in range(3)]
                b_a = [Buf() for _ in range(3)]
                R_t = [sb(st, "R%d" % i, [128, 512], BF16) for i in range(2)]
                b_R = [Buf(), Buf()]
                og = [sb(st, "og%d" % i, [64, 512], BF16) for i in range(2)]
                b_og = [Buf(), Buf()]
                items = []
                hidx = 0
                gcount = -1
                for s in range(NSEQ):
                    for h in range(16):
                        for qt in range(NT):
                            nb = 4 * qt + 4
                            gcount += 1
                            rpar = 0
                            for n in range(nb // 2):
                                kbA = nb - 1 - 2 * n
                                items.append(dict(s=s, h=h, hidx=hidx, qt=qt, kbA=kbA, kbB=kbA - 1, first=(n == 0),
                                                  last=(n == nb // 2 - 1), jA=(kbA - 4 * qt) if kbA >= 4 * qt else None,
                                                  newhead=(qt == 0 and n == 0), newseq=(h == 0 and qt == 0 and n == 0),
                                                  g=gcount, rcur=rpar))
                                rpar ^= 1
                        hidx += 1
                n = len(items)
                allT = lambda s: b_projT[s]
                allV = lambda s: b_projV[s]

                def load_head(s, h, hidx):
                    sl = hidx % 2
                    kb.dma("sp", qh[sl][:], projT[s, h * 64:(h + 1) * 64, :], reads=allT(s), writes=[b_qk[sl]])
                    kb.dma("sp", kh[sl][:], projT[s, 1024 + h * 64:1024 + (h + 1) * 64, :], reads=allT(s), writes=[b_qk[sl]])

                def mask_ap(j):
                    c0 = M_SBMASK + 384 - 128 * j
                    return cm[:, c0:c0 + 512]

                def stA(i):
                    it = items[i]
                    s, h, qt = it["s"], it["h"], it["qt"]
                    sl = it["hidx"] % 2
                    if it["newseq"]:
                        vsrc = projV[s].rearrange("(b p) d -> p b d", p=128)
                        step = max(1, NBLK // 4)
                        for b0 in range(0, NBLK, step):
                            kb.dma("sp", vall[:, b0:b0 + step, :], vsrc[:, b0:b0 + step, :], reads=allV(s), writes=[b_v])
                    if it["newhead"]:
                        if i == 0:
                            load_head(s, h, it["hidx"])
                        nh = it["hidx"] + 1
                        if nh < NSEQ * 16:
                            load_head(nh // 16, nh % 16, nh)
                    w = i % 3
                    bw2 = [b_ps[2 * w], b_ps[2 * w + 1]]
                    for half, kbk in ((0, it["kbA"]), (1, it["kbB"])):
                        kb.op("pe", lambda: nc.tensor.matmul(ps[2 * w + half][:], lhsT=kh[sl][:, kbk * 128:(kbk + 1) * 128],
                                                              rhs=qh[sl][:, qt * 512:(qt + 1) * 512], start=True, stop=False),
                              reads=[b_qk[sl]], writes=[b_ps[2 * w + half]])
                    kb.op("act", lambda: nc.scalar.activation(out=e_t[i % 2][:], in_=psw[w][:], func=AF.Exp),
                          reads=bw2, writes=[b_e[i % 2]])
                    kb.op("act", lambda: nc.scalar.activation(out=sp_t[i % 2][:], in_=e_t[i % 2][:], func=AF.Ln, bias=1.0),
                          reads=[b_e[i % 2]], writes=[b_sp[i % 2]])
                    if it["jA"] is not None:
                        for half, j in ((0, it["jA"]), (1, it["jA"] - 1)):
                            kb.op("dve", lambda: nc.vector.tensor_tensor(out=sp_t[i % 2][:, half * 512:(half + 1) * 512],
                                                                          in0=sp_t[i % 2][:, half * 512:(half + 1) * 512],
                                                                          in1=mask_ap(j), op=ALU.mult),
                                  reads=[b_sp[i % 2], b_cm], writes=[b_sp[i % 2]])

                def stB(i):
                    it = items[i]
                    w = i % 3
                    rc = it["rcur"]
                    first = it["first"]
                    spA = sp_t[i % 2][:, 0:512]
                    spB = sp_t[i % 2][:, 512:1024]
                    bA, bB = 2 * w, 2 * w + 1
                    kb.op("pe", lambda: nc.tensor.matmul(ps[bA][:], lhsT=negtri, rhs=spA, start=False, stop=first),
                          reads=[b_sp[i % 2], b_cm], writes=[b_ps[bA]])
                    if not first:
                        kb.op("pe", lambda: nc.tensor.matmul(ps[bA][:], lhsT=negone, rhs=R_t[rc][:], start=False, stop=True),
                              reads=[b_R[rc], b_cm], writes=[b_ps[bA]])
                    kb.op("pe", lambda: nc.tensor.matmul(ps[bB][:], lhsT=negtri, rhs=spB, start=False, stop=False),
                          reads=[b_sp[i % 2], b_cm], writes=[b_ps[bB]])
                    kb.op("pe", lambda: nc.tensor.matmul(ps[bB][:], lhsT=negone, rhs=spA, start=False, stop=first),
                          reads=[b_sp[i % 2], b_cm], writes=[b_ps[bB]])
                    if not first:
                        kb.op("pe", lambda: nc.tensor.matmul(ps[bB][:], lhsT=negone, rhs=R_t[rc][:], start=False, stop=True),
                              reads=[b_R[rc], b_cm], writes=[b_ps[bB]])
                    if not it["last"]:
                        if first:
                            kb.op("dve", lambda: nc.vector.tensor_tensor(out=R_t[rc ^ 1][:], in0=spA, in1=spB, op=ALU.add),
                                  reads=[b_sp[i % 2]], writes=[b_R[rc ^ 1]])
                        else:
                            kb.op("dve", lambda: nc.vector.tensor_tensor(out=R_t[rc ^ 1][:], in0=R_t[rc][:], in1=spA, op=ALU.add),
                                  reads=[b_sp[i % 2], b_R[rc]], writes=[b_R[rc ^ 1]])
                            kb.op("dve", lambda: nc.vector.tensor_tensor(out=R_t[rc ^ 1][:], in0=R_t[rc ^ 1][:], in1=spB, op=ALU.add),
                                  reads=[b_sp[i % 2], b_R[rc ^ 1]], writes=[b_R[rc ^ 1]])
                    kb.op("act", lambda: nc.scalar.activation(out=a_t[i % 3][:], in_=psw[w][:], func=AF.Exp),
                          reads=[b_ps[bA], b_ps[bB]], writes=[b_a[i % 3]])
                    if it["jA"] is not None:
                        for half, j in ((0, it["jA"]), (1, it["jA"] - 1)):
                            kb.op("dve", lambda: nc.vector.tensor_tensor(out=a_t[i % 3][:, half * 512:(half + 1) * 512],
                                                                          in0=a_t[i % 3][:, half * 512:(half + 1) * 512],
                                                                          in1=mask_ap(j), op=ALU.mult),
                                  reads=[b_a[i % 3], b_cm], writes=[b_a[i % 3]])

                def stC(i):
                    it = items[i]
                    s, h, qt = it["s"], it["h"], it["qt"]
                    ob = 6 + (it["g"] % 2)
                    for half, kbk in ((0, it["kbA"]), (1, it["kbB"])):
                        kb.op("pe", lambda: nc.tensor.matmul(ps[ob][0:64, :], lhsT=vall[:, kbk, h * 64:(h + 1) * 64],
                                                              rhs=a_t[i % 3][:, half * 512:(half + 1) * 512],
                                                              start=(it["first"] and half == 0), stop=(it["last"] and half == 1)),
                              reads=[b_v, b_a[i % 3]], writes=[b_ps[ob]])
                    if it["last"]:
                        g = it["g"] % 2
                        kb.op("dve", lambda: nc.vector.tensor_copy(og[g][:], ps[ob][0:64, :]), reads=[b_ps[ob]], writes=[b_og[g]])
                        kb.dma("sp", oT[s, h * 64:(h + 1) * 64, qt * 512:(qt + 1) * 512], og[g][:],
                               reads=[b_og[g]], writes=[b_oT[s][qt]])

                for step in range(n + 2):
                    if step < n:
                        stA(step)
                    if 0 <= step - 1 < n:
                        stB(step - 1)
                    if 0 <= step - 2 < n:
                        stC(step - 2)
                kb.barrier()

        def phase_even_attn(layer):
            e = layer // 2
            lam_init = 0.8 - 0.6 * math.exp(-0.3 * layer)
            with ExitStack() as st:
                TB = sb(st, "TB", [128, 4, 1024], F32)
                b_TB = Buf()
                TA = sb(st, "TA", [128, 8, 512], F32)
                b_TA = Buf()
                ngB = sb(st, "ngB", [128, 1024], F32)
                ngA = sb(st, "ngA", [128, 512], F32)
                b_ng = Buf()
                sm = sb(st, "sm", [128, 32], F32)
                b_sm = Buf()
                lt = sb(st, "lt", [128, 2, 64], F32)
                b_lt = Buf()
                kb.dma("sp", TB[:], biasB_d[:, :, :], writes=[b_TB])
                kb.dma("sp", TA[:], biasA_d[:, :, :], writes=[b_TA])
                kb.dma("sp", ngB[:], negB_d[:, :], writes=[b_ng])
                kb.dma("sp", ngA[:], negA_d[:, :], writes=[b_ng])
                for h in range(4):
                    kb.op("dve", lambda: nc.vector.tensor_tensor(out=TB[:, h, :], in0=TB[:, h, :], in1=ngB[:], op=ALU.add),
                          reads=[b_TB, b_ng], writes=[b_TB])
                for h in range(8):
                    kb.op("dve", lambda: nc.vector.tensor_tensor(out=TA[:, h, :], in0=TA[:, h, :], in1=ngA[:], op=ALU.add),
                          reads=[b_TA, b_ng], writes=[b_TA])
                kb.op("act", lambda: nc.scalar.activation(out=sm[:, 0:8], in_=cf[:, C_SINK + e * 8:C_SINK + e * 8 + 8], func=AF.Exp),
                      reads=[b_cf], writes=[b_sm])
                lq = cf[:, C_LAM + e * 256:C_LAM + e * 256 + 256].rearrange("p (a d) -> p a d", a=4)
                kb.op("dve", lambda: nc.vector.tensor_tensor(out=lt[:, 0, :], in0=lq[:, 0, :], in1=lq[:, 1, :], op=ALU.mult),
                      reads=[b_cf], writes=[b_lt])
                kb.op("dve", lambda: nc.vector.tensor_tensor(out=lt[:, 1, :], in0=lq[:, 2, :], in1=lq[:, 3, :], op=ALU.mult),
                      reads=[b_cf], writes=[b_lt])
                kb.op("dve", lambda: nc.vector.tensor_reduce(out=sm[:, 8:10], in_=lt[:], axis=AX.X, op=ALU.add),
                      reads=[b_lt], writes=[b_sm])
                kb.op("act", lambda: nc.scalar.activation(out=sm[:, 12:14], in_=sm[:, 8:10], func=AF.Exp),
                      reads=[b_sm], writes=[b_sm])
                kb.op("dve", lambda: nc.vector.scalar_tensor_tensor(out=sm[:, 10:11], in0=sm[:, 13:14], scalar=-lam_init,
                                                                     in1=sm[:, 12:13], op0=ALU.add, op1=ALU.subtract),
                      reads=[b_sm], writes=[b_sm])
                kb.op("dve", lambda: nc.vector.tensor_scalar(sm[:, 11:12], cf[:, C_DNORM + e:C_DNORM + e + 1],
                                                              1.0 - lam_init, None, ALU.mult),
                      reads=[b_cf, b_sm], writes=[b_sm])

                avall = sb(st, "avall", [128, NBLK, 128], BF16)
                b_av = Buf()
                bvall = sb(st, "bvall", [128, NBLK, 512], BF16)
                b_bv = Buf()
                qh = [sb(st, "qh%d" % i, [64, S], BF16) for i in range(2)]
                b_q = [Buf(), Buf()]
                kh = [sb(st, "kh%d" % i, [64, S], BF16) for i in range(2)]
                b_k = [Buf(), Buf()]
                tmp = [sb(st, "tmp%d" % i, [128, 512], F32) for i in range(2)]
                b_tmp = [Buf(), Buf()]
                p_t = [sb(st, "p%d" % i, [128, 512], BF16) for i in range(3)]
                b_p = [Buf() for _ in range(3)]
                dtmp = [sb(st, "dtmp%d" % i, [128, 512], F32) for i in range(3)]
                b_dtmp = [Buf() for _ in range(3)]
                dp = [sb(st, "dp%d" % i, [128, 512], BF16) for i in range(4)]
                b_dp = [Buf() for _ in range(4)]
                og = [sb(st, "og%d" % i, [128, 512], BF16) for i in range(2)]
                b_og = [Buf(), Buf()]
                kh2 = sb(st, "kh2", [128, S], BF16)
                b_kh2 = Buf()
                qz = [sb(st, "qz%d" % i, [128, S], BF16) for i in range(2)]
                b_qz = [Buf(), Buf()]
                kb.op("pool", lambda: nc.gpsimd.memset(qz[0][64:128, :], 0.0), writes=[b_qz[0]])
                kb.op("pool", lambda: nc.gpsimd.memset(qz[1][0:64, :], 0.0), writes=[b_qz[1]])
                swf = [sb(st, "swf%d" % i, [64, 256], F32) for i in range(2)]
                b_swf = [Buf(), Buf()]
                dacc = [[sb(st, "dacc%d_%d" % (k, r), [128, 512], F32) for r in range(2)] for k in range(2)]
                b_dacc = [[Buf(), Buf()], [Buf(), Buf()]]
                dacp = [[sb(st, "dacp%d_%d" % (k, r), [128, 512], F32) for r in range(2)] for k in range(2)]
                b_dacp = [[Buf(), Buf()], [Buf(), Buf()]]
                daccb = [[sb(st, "daccb%d_%d" % (k, r), [128, 512], BF16) for r in range(2)] for k in range(2)]
                b_daccb = [[Buf(), Buf()], [Buf(), Buf()]]
                fin2 = [[sb(st, "fin2_%d_%d" % (k, i), [128, 512], F32) for i in range(4)] for k in range(2)]
                b_fin2 = [[Buf() for _ in range(4)] for _ in range(2)]
                sqd2 = [sb(st, "sqd2_%d" % k, [128, 512], BF16) for k in range(2)]
                b_sqd2 = [Buf(), Buf()]

                qi = 0
                ki = 0
                it_i = 0
                for s in range(NSEQ):
                    vsrc = projV[s].rearrange("(b p) d -> p b d", p=128)
                    kb.dma("sp", avall[:], vsrc[:, :, 0:128], reads=b_projV[s], writes=[b_av])
                    step = max(1, NBLK // 4)
                    for b0 in range(0, NBLK, step):
                        kb.dma("sp", bvall[:, b0:b0 + step, :], vsrc[:, b0:b0 + step, 128:640], reads=b_projV[s], writes=[b_bv])
                    sitems = [(g, hh, qp) for g in range(2) for hh in range(4) for qp in range(NBLK // 2)]
                    ns_ = len(sitems)
                    sw_slots = {}

                    def sA(i):
                        g, hh, qp = sitems[i]
                        h = g * 4 + hh
                        if hh == 0 and qp == 0:
                            kb.dma("sp", kh[g][:], projT[s, 512 + g * 64:512 + (g + 1) * 64, :], reads=b_projT[s], writes=[b_k[g]])
                        if qp == 0:
                            kb.dma("sp", qh[h % 2][:], projT[s, h * 64:(h + 1) * 64, :], reads=b_projT[s], writes=[b_q[h % 2]])
                        qs, ks = h % 2, g
                        zb, tp, pp = i % 2, i % 2, i % 3
                        for u in range(2):
                            qb = 2 * qp + u
                            qsl = qh[qs][:, qb * 128:(qb + 1) * 128]
                            if qb > 0:
                                kb.op("pe", lambda: nc.tensor.matmul(ps[zb][:, u * 256:u * 256 + 128],
                                                                      lhsT=kh[ks][:, (qb - 1) * 128:qb * 128], rhs=qsl,
                                                                      start=True, stop=True),
                                      reads=[b_k[ks], b_q[qs]], writes=[b_ps[zb]])
                            kb.op("pe", lambda: nc.tensor.matmul(ps[zb][:, u * 256 + 128:u * 256 + 256],
                                                                  lhsT=kh[ks][:, qb * 128:(qb + 1) * 128], rhs=qsl,
                                                                  start=True, stop=True),
                                  reads=[b_k[ks], b_q[qs]], writes=[b_ps[zb]])
                        c0 = 128 if qp == 0 else 0
                        kb.op("dve", lambda: nc.vector.tensor_tensor(out=tmp[tp][:, c0:512], in0=ps[zb][:, c0:512],
                                                                      in1=TA[:, h, c0:512], op=ALU.add),
                              reads=[b_ps[zb], b_TA], writes=[b_tmp[tp]])
                        kb.op("act", lambda: nc.scalar.activation(out=p_t[pp][:, c0:512], in_=tmp[tp][:, c0:512], func=AF.Exp),
                              reads=[b_tmp[tp]], writes=[b_p[pp]])

                    def sB(i):
                        g, hh, qp = sitems[i]
                        pp = i % 3
                        nb = 2 + i % 2
                        db = 4 + i % 2
                        for u in range(2):
                            qb = 2 * qp + u
                            first = True
                            for half, kblk in ((0, qb - 1), (1, qb)):
                                if kblk < 0:
                                    continue
                                pc = u * 256 + half * 128
                                lastm = (half == 1)
                                kb.op("pe", lambda: nc.tensor.matmul(ps[nb][0:64, u * 128:(u + 1) * 128],
                                                                      lhsT=avall[:, kblk, g * 64:(g + 1) * 64],
                                                                      rhs=p_t[pp][:, pc:pc + 128], start=first, stop=lastm),
                                      reads=[b_av, b_p[pp]], writes=[b_ps[nb]])
                                kb.op("pe", lambda: nc.tensor.matmul(ps[db][0:64, u * 128:(u + 1) * 128],
                                                                      lhsT=one_m[:, 0:64],
                                                                      rhs=p_t[pp][:, pc:pc + 128], start=first, stop=lastm),
                                      reads=[b_cm, b_p[pp]], writes=[b_ps[db]])
                                first = False

                    def sC1(i):
                        g, hh, qp = sitems[i]
                        h = g * 4 + hh
                        db = 4 + i % 2
                        f0 = swf[i % 2]
                        bf0 = b_swf[i % 2]
                        kb.op("act", lambda: nc.scalar.activation(out=f0[0:64, :], in_=ps[db][0:64, 0:256], func=AF.Ln,
                                                                   bias=sm[0:64, h:h + 1]),
                              reads=[b_ps[db], b_sm], writes=[bf0])
                        kb.op("act", lambda: nc.scalar.activation(out=f0[0:64, :], in_=f0[0:64, :], func=AF.Exp, scale=-1.0),
                              reads=[bf0], writes=[bf0])

                    def sC2(i):
                        g, hh, qp = sitems[i]
                        h = g * 4 + hh
                        nb = 2 + i % 2
                        f0 = swf[i % 2]
                        bf0 = b_swf[i % 2]
                        oslot = (qp // 2) % 2
                        ocol = (qp % 2) * 256
                        kb.op("dve", lambda: nc.vector.tensor_tensor(out=og[oslot][0:64, ocol:ocol + 256],
                                                                      in0=ps[nb][0:64, 0:256], in1=f0[0:64, :],
                                                                      op=ALU.mult),
                              reads=[b_ps[nb], bf0], writes=[b_og[oslot]])
                        if qp % 2 == 1:
                            qt = qp // 2
                            kb.dma("sp", oT[s, h * 64:(h + 1) * 64, qt * 512:(qt + 1) * 512], og[oslot][0:64, :],
                                   reads=[b_og[oslot]], writes=[b_oT[s][qt]])

                    for step in range(ns_ + 2):
                        if 0 <= step - 2 < ns_:
                            sC1(step - 2)
                        if step < ns_:
                            sA(step)
                        if 0 <= step - 1 < ns_:
                            sB(step - 1)
                        if 0 <= step - 2 < ns_:
                            sC2(step - 2)

                    ditems = []
                    gg = -1
                    for h in range(4):
                        for qt in range(NT):
                            gg += 1
                            nbk = 4 * qt + 4
                            for r in range(2):
                                for kbk in range(nbk):
                                    ditems.append((h, qt, r, kbk, nbk, gg))
                    nd = len(ditems)
                    deferred = []

                    def finalize(h, qt, g, step):
                        k = g % 2
                        dset = (6, 7)
                        fa = fin2[k]
                        bfa = b_fin2[k]

                        def F1():
                            for r in range(2):
                                kb.op("act", lambda: nc.scalar.activation(out=fa[2 + r][:], in_=ps[dset[r]][:], func=AF.Ln),
                                      reads=[b_ps[dset[r]]], writes=[bfa[2 + r]])

                        def F2():
                            for r in range(2):
                                kb.op("act", lambda: nc.scalar.activation(out=fa[2 + r][:], in_=fa[2 + r][:], func=AF.Exp, scale=-1.0),
                                      reads=[bfa[2 + r]], writes=[bfa[2 + r]])
                            for r in range(2):
                                kb.op("dve", lambda: nc.vector.tensor_tensor(out=fa[r][:], in0=fa[r][:], in1=fa[2 + r][:], op=ALU.mult),
                                      reads=[bfa[r], bfa[2 + r]], writes=[bfa[r]])
                            kb.op("dve", lambda: nc.vector.scalar_tensor_tensor(out=fa[2][:], in0=fa[1][:], scalar=sm[:, 10:11],
                                                                                 in1=fa[0][:], op0=ALU.mult, op1=ALU.add),
                                  reads=[bfa[0], bfa[1], bfa[2], b_sm], writes=[bfa[2]])
                            kb.op("act", lambda: nc.scalar.activation(out=sqd2[k][:], in_=fa[2][:], func=AF.Square),
                                  reads=[bfa[2]], writes=[b_sqd2[k]])

                        def F3():
                            kb.op("pe", lambda: nc.tensor.matmul(ps[2][:], lhsT=one_128, rhs=sqd2[k][:], start=True, stop=True),
                                  reads=[b_cm, b_sqd2[k]], writes=[b_ps[2]])

                        def F4():
                            rsqrt_op(fa[3][:], ps[2][:], b_ps[2], bfa[3])
                            kb.op("dve", lambda: nc.vector.scalar_tensor_tensor(out=og[k][:], in0=fa[2][:], scalar=sm[:, 11:12],
                                                                                 in1=fa[3][:], op0=ALU.mult, op1=ALU.mult),
                                  reads=[bfa[2], bfa[3], b_sm], writes=[b_og[k]])
                            kb.dma("sp", oT[s, 512 + h * 128:512 + (h + 1) * 128, qt * 512:(qt + 1) * 512], og[k][:],
                                   reads=[b_og[k]], writes=[b_oT[s][qt]])

                        F1()
                        deferred.append((step + 2, F2))
                        deferred.append((step + 4, F3))
                        deferred.append((step + 6, F4))

                    def dA(i):
                        h, qt, r, kbk, nbk, g = ditems[i]
                        if qt == 0 and r == 0 and kbk == 0:
                            row = h * 128
                            kb.dma("sp", kh2[:], projT[s, 1152 + row:1152 + row + 128, :], reads=b_projT[s], writes=[b_kh2])
                            kb.dma("sp", qz[0][0:64, :], projT[s, 640 + row:640 + row + 64, :], reads=b_projT[s], writes=[b_qz[0]])
                            kb.dma("sp", qz[1][64:128, :], projT[s, 640 + row + 64:640 + row + 128, :], reads=b_projT[s],
                                   writes=[b_qz[1]])
                        zb = (0, 1, 3)[i % 3]
                        tp = i % 3
                        pp = i % 4
                        if qt == 0 and r == 0 and kbk == 0:
                            for _ in range(N_KICK):
                                kb.op("pe", lambda: nc.tensor.matmul(ps[zb][:], lhsT=one_m, rhs=cm[:, 0:512], start=True, stop=True),
                                      reads=[b_cm], writes=[b_ps[zb]])
                        kb.op("pe", lambda: nc.tensor.matmul(ps[zb][:], lhsT=kh2[:, kbk * 128:(kbk + 1) * 128],
                                                              rhs=qz[r][:, qt * 512:(qt + 1) * 512], start=True, stop=True),
                              reads=[b_kh2, b_qz[r]], writes=[b_ps[zb]])
                        delta = qt * 512 - kbk * 128
                        if delta <= 128:
                            c0 = delta + 384
                            kb.op("dve", lambda: nc.vector.tensor_tensor(out=dtmp[tp][:], in0=ps[zb][:],
                                                                          in1=TB[:, h, c0:c0 + 512], op=ALU.add),
                                  reads=[b_ps[zb], b_TB], writes=[b_dtmp[tp]])
                            kb.op("act", lambda: nc.scalar.activation(out=dp[pp][:], in_=dtmp[tp][:], func=AF.Exp),
                                  reads=[b_dtmp[tp]], writes=[b_dp[pp]])
                        else:
                            kb.op("act", lambda: nc.scalar.activation(out=dp[pp][:], in_=ps[zb][:], func=AF.Exp,
                                                                       bias=cfc(C_FARB + h)),
                                  reads=[b_ps[zb], b_cf], writes=[b_dp[pp]])

                    def dB(i, step):
                        h, qt, r, kbk, nbk, g = ditems[i]
                        pp = i % 4
                        pvb = 4 + r
                        dnb = (6, 7)[r]
                        kb.op("pe", lambda: nc.tensor.matmul(ps[pvb][:], lhsT=bvall[:, kbk, h * 128:(h + 1) * 128],
                                                              rhs=dp[pp][:], start=(kbk == 0), stop=(kbk == nbk - 1)),
                              reads=[b_bv, b_dp[pp]], writes=[b_ps[pvb]])
                        k = g % 2
                        if True:
                            if kbk == 0:
                                kb.op("dve", lambda: nc.vector.tensor_copy(dacc[k][r][:], dp[pp][:]),
                                      reads=[b_dp[pp]], writes=[b_dacc[k][r]])
                            else:
                                kb.op("dve", lambda: nc.vector.tensor_tensor(out=dacc[k][r][:], in0=dacc[k][r][:], in1=dp[pp][:],
                                                                              op=ALU.add),
                                      reads=[b_dp[pp], b_dacc[k][r]], writes=[b_dacc[k][r]])
                        else:
                            if kbk == 1:
                                kb.op("pool", lambda: nc.gpsimd.tensor_copy(dacp[k][r][:], dp[pp][:]),
                                      reads=[b_dp[pp]], writes=[b_dacp[k][r]])
                            else:
                                kb.op("pool", lambda: nc.gpsimd.tensor_tensor(out=dacp[k][r][:], in0=dacp[k][r][:], in1=dp[pp][:],
                                                                               op=ALU.add),
                                      reads=[b_dp[pp], b_dacp[k][r]], writes=[b_dacp[k][r]])
                        if kbk == nbk - 1:
                            kb.op("act", lambda: nc.scalar.copy(daccb[k][r][:], dacc[k][r][:]),
                                  reads=[b_dacc[k][r]], writes=[b_daccb[k][r]])

                            def den_mm(k=k, r=r, dnb=dnb):
                                kb.op("pe", lambda: nc.tensor.matmul(ps[dnb][:], lhsT=one_m, rhs=daccb[k][r][:], start=True, stop=True),
                                      reads=[b_cm, b_daccb[k][r]], writes=[b_ps[dnb]])
                            deferred.append((step + 1, den_mm))
                            if r == 1:
                                for rr in range(2):
                                    kb.op("dve", lambda: nc.vector.tensor_copy(fin2[k][rr][:], ps[4 + rr][:]),
                                          reads=[b_ps[4 + rr]], writes=[b_fin2[k][rr]])
                                deferred.append((step + 2, lambda: finalize(h, qt, g, step + 2)))

                    for step in range(nd + 2):
                        if step < nd:
                            dA(step)
                        if step >= 2:
                            dB(step - 2, step)
                        while True:
                            deferred.sort(key=lambda e: e[0])
                            if not (deferred and deferred[0][0] <= step):
                                break
                            deferred.pop(0)[1]()
                    while deferred:
                        deferred.sort(key=lambda e: e[0])
                        deferred.pop(0)[1]()
                kb.barrier()

        def phase_outproj(layer, xsrc, first_layer, mid_cb=None):
            w_ap = (w_out_even if layer % 2 == 0 else w_out_odd)[layer // 2]
            TP = 256
            with ExitStack() as st:
                wb, bw = load_weight(st, "wout", w_ap, D, D)
                if mid_cb is not None:
                    mid_cb()
                xin = [sb(st, "xin%d" % i, [128, 8, TP], F32) for i in range(2)]
                b_xin = [Buf(), Buf()]
                oin = [sb(st, "oin%d" % i, [128, 8, TP], BF16) for i in range(2)]
                b_oin = [Buf(), Buf()]
                tiles = [(s, t) for s in range(NSEQ) for t in range(S // TP)]

                def load(i):
                    s, t = tiles[i]
                    src = xsrc[s].rearrange("(c p) n -> p c n", p=128)[:, :, t * TP:(t + 1) * TP]
                    kb.dma("sp", xin[i % 2][:], src, reads=([] if first_layer else [b_xs[s][t]]), writes=[b_xin[i % 2]])
                    osrc = oT[s].rearrange("(c p) n -> p c n", p=128)[:, :, t * TP:(t + 1) * TP]
                    kb.dma("sp", oin[i % 2][:], osrc, reads=[b_oT[s][t // 2]], writes=[b_oin[i % 2]])

                load(0)
                pb = 0
                for i, (s, t) in enumerate(tiles):
                    if i + 1 < len(tiles):
                        load(i + 1)
                    x_, bx_ = xin[i % 2], b_xin[i % 2]
                    o_, bo_ = oin[i % 2], b_oin[i % 2]
                    for oc in range(8):
                        p = pb % 8
                        pb += 1
                        for k in range(8):
                            kb.op("pe", lambda: nc.tensor.matmul(ps[p][:, 0:TP], lhsT=wb[:, k, oc * 128:(oc + 1) * 128], rhs=o_[:, k, :],
                                                                  start=(k == 0), stop=(k == 7)),
                                  reads=[bw, bo_], writes=[b_ps[p]])
                        kb.op("dve", lambda: nc.vector.tensor_tensor(out=x_[:, oc, :], in0=x_[:, oc, :], in1=ps[p][:, 0:TP], op=ALU.add),
                              reads=[b_ps[p], bx_], writes=[bx_])
                    dst = xs[s].rearrange("(c p) n -> p c n", p=128)[:, :, t * TP:(t + 1) * TP]
                    kb.dma("sp", dst, x_[:], reads=[bx_], writes=[b_xs[s][t]])
                kb.barrier()

        def phase_ffn(layer, final, wts):
            TF = 256
            NTF = S // TF
            wu, bwu, wd, bwd = wts
            with ExitStack() as st:
                xin = [sb(st, "xin%d" % i, [128, 8, TF], F32) for i in range(3)]
                b_xin = [Buf() for _ in range(3)]
                hT2 = [sb(st, "hT%d" % i, [128, 8, TF], BF16) for i in range(2)]
                b_hT2 = [Buf(), Buf()]
                sq = [sb(st, "sq%d" % i, [128, TF], BF16) for i in range(2)]
                b_sq = [Buf(), Buf()]
                rstd = sb(st, "rstd", [128, TF], F32)
                b_rstd = Buf()
                G = [sb(st, "G%d" % i, [128, 22, TF], BF16) for i in range(2)]
                b_G = [Buf(), Buf()]
                carry = sb(st, "carry", [128, 22, 2, 2], F32)
                b_carry = [Buf() for _ in range(22)]
                ub = [sb(st, "ub%d" % i, [128, 2, TF + 2], F32) for i in range(2)]
                b_ub = [Buf(), Buf()]
                b_ubc = [Buf(), Buf()]
                cg = [sb(st, "cg%d" % i, [128, TF], F32) for i in range(2)]
                b_cg = [Buf(), Buf()]
                cv = [sb(st, "cv%d" % i, [128, TF], F32) for i in range(2)]
                b_cv = [Buf(), Buf()]
                sg = [sb(st, "sg%d" % i, [128, TF], F32) for i in range(2)]
                b_sg = [Buf(), Buf()]
                tiles = [(s, t) for s in range(NSEQ) for t in range(NTF)]

                def load(i):
                    s, t = tiles[i]
                    src = xs[s].rearrange("(c p) n -> p c n", p=128)[:, :, t * TF:(t + 1) * TF]
                    kb.dma("sp", xin[i % 3][:], src, reads=[b_xs[s][t]], writes=[b_xin[i % 3]])

                def cw(tap, ch):
                    return cfc(C_CONVW + layer * 132 + tap * 44 + ch)

                def cb(ch):
                    return cfc(C_CONVB + layer * 44 + ch)

                def down_work(i):
                    s, t = tiles[i]
                    x_, bx_ = xin[i % 3], b_xin[i % 3]
                    G_, bG_ = G[i % 2], b_G[i % 2]
                    work = []
                    for oc in range(8):
                        p = 3 + oc % 4
                        for k in range(22):
                            def mm(oc=oc, k=k, p=p):
                                kb.op("pe", lambda: nc.tensor.matmul(ps[p][:, 0:TF], lhsT=wd[:, k, oc * 128:(oc + 1) * 128],
                                                                      rhs=G_[:, k, :], start=(k == 0), stop=(k == 21)),
                                      reads=[bwd, bG_], writes=[b_ps[p]])
                                if k == 21:
                                    kb.op("dve", lambda: nc.vector.tensor_tensor(out=x_[:, oc, :], in0=x_[:, oc, :],
                                                                                  in1=ps[p][:, 0:TF], op=ALU.add),
                                          reads=[b_ps[p], bx_], writes=[bx_])
                            work.append(mm)

                    def post():
                        if not final:
                            dst = xs[s].rearrange("(c p) n -> p c n", p=128)[:, :, t * TF:(t + 1) * TF]
                            kb.dma("sp", dst, x_[:], reads=[bx_], writes=[b_xs[s][t]])
                        else:
                            for c in range(8):
                                q = c % 2
                                kb.op("act", lambda: nc.scalar.activation(out=sq[q][:], in_=x_[:, c, :], func=AF.Square),
                                      reads=[bx_], writes=[b_sq[q]])
                                kb.op("pe", lambda: nc.tensor.matmul(ps[7][:, 0:TF], lhsT=one_d, rhs=sq[q][:],
                                                                      start=(c == 0), stop=(c == 7)),
                                      reads=[b_sq[q], b_cm], writes=[b_ps[7]])
                            rsqrt_op(rstd[:], ps[7][:, 0:TF], b_ps[7], b_rstd)
                            for c in range(8):
                                kb.op("dve", lambda: nc.vector.scalar_tensor_tensor(
                                    out=x_[:, c, :], in0=x_[:, c, :], scalar=cfc(C_NFIN + c), in1=rstd[:],
                                    op0=ALU.mult, op1=ALU.mult), reads=[bx_, b_rstd, b_cf], writes=[bx_])
                            dst = yT[s].rearrange("(c p) n -> p c n", p=128)[:, :, t * TF:(t + 1) * TF]
                            kb.dma("sp", dst, x_[:], reads=[bx_], writes=[b_y])
                    return work, post

                load(0)
                pb = 0
                ui = 0
                pending, pending_post = [], None

                def drain(nmax):
                    nonlocal pending, pending_post
                    for _ in range(min(nmax, len(pending))):
                        pending.pop(0)()
                    if not pending and pending_post is not None:
                        pending_post()
                        pending_post = None

                for i, (s, t) in enumerate(tiles):
                    if i + 1 < len(tiles):
                        load(i + 1)
                    x_, bx_ = xin[i % 3], b_xin[i % 3]
                    G_, bG_ = G[i % 2], b_G[i % 2]
                    if t == 0:
                        kb.op("pool", lambda: nc.gpsimd.memset(carry[:], 0.0), writes=b_carry)
                    drain(22)
                    hT, b_hT = hT2[i % 2], b_hT2[i % 2]
                    if i == 0:
                        rmsnorm_tile(x_, bx_, hT, b_hT, sq, b_sq, rstd, b_rstd, 7, C_NFFN + layer * 8, TF)
                    for c in range(22):
                        if c == 11 and i + 1 < len(tiles):
                            rmsnorm_tile(xin[(i + 1) % 3], b_xin[(i + 1) % 3], hT2[(i + 1) % 2], b_hT2[(i + 1) % 2],
                                         sq, b_sq, rstd, b_rstd, 7, C_NFFN + layer * 8, TF)
                        p = pb % 3
                        pb += 1
                        u_ = ui % 2
                        ui += 1
                        for half, col in ((0, c * 128), (1, DFF + c * 128)):
                            for k in range(8):
                                kb.op("pe", lambda: nc.tensor.matmul(ps[p][:, half * TF:(half + 1) * TF],
                                                                      lhsT=wu[:, k, col:col + 128], rhs=hT[:, k, :],
                                                                      start=(k == 0), stop=(k == 7)),
                                      reads=[bwu, b_hT], writes=[b_ps[p]])
                        drain(7)
                        U = ub[u_]
                        bU = b_ub[u_]
                        bUc = b_ubc[u_]
                        kb.op("pool", lambda: nc.gpsimd.tensor_copy(U[:, :, 0:2], carry[:, c, :, :]),
                              reads=[b_carry[c]], writes=[bUc])
                        kb.op("act", lambda: nc.scalar.copy(U[:, :, 2:TF + 2], ps[p][:, :].rearrange("p (a n) -> p a n", a=2)),
                              reads=[b_ps[p]], writes=[bU])
                        kb.op("pool", lambda: nc.gpsimd.tensor_copy(carry[:, c, :, :], U[:, :, TF:TF + 2]),
                              reads=[bU], writes=[b_carry[c]])
                        for half, ch, dst, bdst in ((0, c, cg[u_], b_cg[u_]), (1, 22 + c, cv[u_], b_cv[u_])):
                            kb.op("act", lambda: nc.scalar.activation(out=dst[:], in_=ps[p][:, half * TF:(half + 1) * TF],
                                                                       func=AF.Identity, scale=cw(2, ch), bias=cb(ch)),
                                  reads=[b_ps[p], b_cf], writes=[bdst])
                            kb.op("dve", lambda: nc.vector.scalar_tensor_tensor(out=dst[:], in0=U[:, half, 1:TF + 1], scalar=cw(1, ch),
                                                                                 in1=dst[:], op0=ALU.mult, op1=ALU.add),
                                  reads=[bU, bUc, bdst, b_cf], writes=[bdst])
                            kb.op("dve", lambda: nc.vector.scalar_tensor_tensor(out=dst[:], in0=U[:, half, 0:TF], scalar=cw(0, ch),
                                                                                 in1=dst[:], op0=ALU.mult, op1=ALU.add),
                                  reads=[bU, bUc, bdst, b_cf], writes=[bdst])
                        kb.op("act", lambda: nc.scalar.activation(out=sg[u_][:], in_=cg[u_][:], func=AF.Silu),
                              reads=[b_cg[u_]], writes=[b_sg[u_]])
                        kb.op("dve", lambda: nc.vector.tensor_tensor(out=G_[:, c, :], in0=sg[u_][:], in1=cv[u_][:], op=ALU.mult),
                              reads=[b_sg[u_], b_cv[u_]], writes=[bG_])
                    drain(10 ** 6)
                    pending, pending_post = down_work(i)
                drain(10 ** 6)
                kb.barrier()

        for layer in range(DEPTH):
            if layer == 0:
                xsrc, bfn = xT, (lambda s, t: [])
            else:
                xsrc, bfn = xs, (lambda s, t: [b_xs[s][2 * t], b_xs[s][2 * t + 1]])
            phase_inproj(layer, xsrc, bfn)
            if layer % 2 == 0:
                phase_even_attn(layer)
            else:
                phase_stickbreak()
            with ExitStack() as fst:
                wu = sb(fst, "wup", [128, 8, 2 * DFF], BF16)
                wd = sb(fst, "wdn", [128, 22, D], BF16)
                bwu, bwd = Buf(), Buf()
                with ExitStack() as sst:
                    pstg = [sb(sst, "pf_s%d" % i, [128, 1408], F32) for i in range(2)]
                    pbst = [Buf(), Buf()]

                    def prefetch():
                        fill_weight(wu, bwu, ffn_up[layer], D, 2 * DFF, pstg, pbst, 1408, q="act", engines=("act", "pool", "act"))
                        fill_weight(wd, bwd, ffn_down[layer], DFF, D, pstg, pbst, 1024, q="act", engines=("act", "pool", "act"))

                    phase_outproj(layer, xsrc, layer == 0, prefetch)
                phase_ffn(layer, (layer == DEPTH - 1), (wu, bwu, wd, bwd))
        kb.barrier()
        print("program: %d instructions, %d waits" % (kb.n_ins, kb.n_wait))
    return nc


def _t5_bucket_np(d):
    n = np.maximum(d, 0)
    nf = np.maximum(n, 1).astype(np.float32)
    large = 16 + (np.log(nf / np.float32(16)) / np.float32(math.log(128 / 16)) * np.float32(16)).astype(np.int32)
    large = np.minimum(large, 31)
    return np.where(n < 16, n, large)


def _static_tables():
    p = np.arange(128)[:, None]
    cmat = np.zeros((128, M_END), np.float32)
    j = np.arange(128)[None, :]
    cmat[:, M_NEGTRI:M_NEGTRI + 128] = -(p >= j).astype(np.float32)
    cmat[:, M_NEGONE:M_NEGONE + 128] = -1.0
    cmat[:, M_ONE:M_ONE + 128] = 1.0
    cmat[:, M_ONE_D:M_ONE_D + 128] = 1.0 / 1024
    cmat[:, M_ONE_128:M_ONE_128 + 128] = 1.0 / 128
    x = np.arange(896)[None, :]
    cmat[:, M_SBMASK:M_SBMASK + 896] = ((x - 384 - p) > 0).astype(np.float32)
    jB = np.arange(1024)[None, :]
    dB = jB - 384 - p
    negB = np.where(dB < 0, NEG, 0.0).astype(np.float32)
    idxB = _t5_bucket_np(dB)
    c = np.arange(128)[None, :]
    d0 = 128 + c - p
    d1 = c - p
    dA = np.concatenate([d0, d1, d0, d1], axis=1)
    negA = np.where((dA < 0) | (dA >= 128), NEG, 0.0).astype(np.float32)
    idxA = _t5_bucket_np(np.clip(dA, 0, 127))
    return cmat, negB, idxB, negA, idxA


def _prep_shared(inp):
    cmat, negB, idxB, negA, idxA = _static_tables()
    rb = np.asarray(inp["rel_bias"], np.float32)
    biasB = np.ascontiguousarray(rb[idxB][:, :, 8:12].transpose(0, 2, 1))
    biasA = np.ascontiguousarray(rb[idxA][:, :, 0:8].transpose(0, 2, 1))
    cf = np.zeros((128, C_END), np.float32)

    def fm(v):
        v = np.asarray(v, np.float32)
        return np.moveaxis(v.reshape(v.shape[:-1] + (8, 128)), -1, 0)

    cf[:, C_NMIX:C_NMIX + 32] = fm(inp["norm_mix"]).reshape(128, 32)
    cf[:, C_NFFN:C_NFFN + 32] = fm(inp["norm_ffn"]).reshape(128, 32)
    cf[:, C_NFIN:C_NFIN + 8] = fm(inp["norm_final"]).reshape(128, 8)
    cw = np.asarray(inp["ffn_conv"], np.float32).reshape(4, 3, 44, 128)
    cf[:, C_CONVW:C_CONVW + 528] = np.moveaxis(cw, -1, 0).reshape(128, 528)
    cbv = np.asarray(inp["ffn_conv_b"], np.float32).reshape(4, 44, 128)
    cf[:, C_CONVB:C_CONVB + 176] = np.moveaxis(cbv, -1, 0).reshape(128, 176)
    cf[:, C_SINK:C_SINK + 16] = np.broadcast_to(np.asarray(inp["sinks"], np.float32).reshape(1, 16), (128, 16))
    cf[:, C_DNORM:C_DNORM + 2] = np.asarray(inp["diff_norm"], np.float32).T
    lam = np.stack([np.asarray(inp[k], np.float32) for k in ("lam_q1", "lam_k1", "lam_q2", "lam_k2")], axis=1)
    cf[:, C_LAM:C_LAM + 512] = np.broadcast_to(lam.reshape(1, 512), (128, 512))
    cf[:, C_FARB:C_FARB + 4] = np.broadcast_to(rb[31, 8:12].reshape(1, 4), (128, 4))
    shared = {
        "w_in_even": np.ascontiguousarray(inp["w_in_even"], np.float32),
        "w_out_even": np.ascontiguousarray(inp["w_out_even"], np.float32),
        "w_in_odd": np.ascontiguousarray(inp["w_in_odd"], np.float32),
        "w_out_odd": np.ascontiguousarray(inp["w_out_odd"], np.float32),
        "ffn_up": np.ascontiguousarray(inp["ffn_up"], np.float32),
        "ffn_down": np.ascontiguousarray(inp["ffn_down"], np.float32),
        "cf32": cf, "cmat": cmat, "biasB": biasB, "negB": negB, "biasA": biasA, "negA": negA,
    }
    return shared


def run(inp, S, NSEQ, DEPTH, n_cores):
    x = np.asarray(inp["x"], np.float32)
    shared = _prep_shared(inp)
    nc = build_program(S, NSEQ, DEPTH)
    in_maps = []
    for c in range(n_cores):
        m = dict(shared)
        m["xT"] = np.ascontiguousarray(x[c * NSEQ:(c + 1) * NSEQ].transpose(0, 2, 1))
        in_maps.append(m)
    res = run_bass_kernel_spmd(nc, in_maps, core_ids=list(range(n_cores)))
    if DEBUG:
        for k in ("xs", "projT", "projV", "oT"):
            DBG[k] = [np.asarray(r[k]) for r in res.results]
    outs = [np.asarray(r["yT"]).transpose(0, 2, 1) for r in res.results]
    return np.ascontiguousarray(np.concatenate(outs, axis=0)).astype(np.float32)


def kernel(**inputs):
    return run(inputs, S=4096, NSEQ=2, DEPTH=4, n_cores=N_CORES)
```
